# Optimizing a Trainium2 kernel written in Bass

```python
import numpy as np
import jax
import jax.numpy as jnp
from jax import lax

D_MODEL = 1024
BATCH = 2
SEQ = 8192
DEPTH = 1

NSA_HEADS = 8
NSA_KV_GROUPS = 2
NSA_HPG = NSA_HEADS // NSA_KV_GROUPS
NSA_HEAD_DIM = 64
N_NSA_BRANCH = 3
CMP_LEN = 32
CMP_STRIDE = 16
CMP_HIDDEN = 128
SLC_LEN = 64
SLC_TOPK = 16
WINDOW = 512
Q_BLOCK = 128
FORCE_SCORE = 1.0e4

GLA_HEADS = 4
GLA_DK = 64
GLA_DV = 128
GLA_GATE_RANK = 16
GLA_GATE_TAU = 16.0
GLA_CHUNK = 64

ROPE_THETA = 500000.0
ROPE_DIM = NSA_HEAD_DIM // 4

D_FF = (8 * D_MODEL + 3 * 256 - 1) // (3 * 256) * 256

EPS = 1e-6
NEG_INF = -1e30

NSA_Q_COLS = NSA_HEADS * NSA_HEAD_DIM
NSA_KV_COLS = N_NSA_BRANCH * 2 * NSA_KV_GROUPS * NSA_HEAD_DIM
NSA_GATE_COLS = NSA_HEADS * N_NSA_BRANCH
GLA_QK_COLS = GLA_HEADS * GLA_DK
GLA_V_COLS = GLA_HEADS * GLA_DV
MERGE_COLS = 2 * D_MODEL
IN_COLS = NSA_Q_COLS + NSA_KV_COLS + NSA_GATE_COLS + 2 * GLA_QK_COLS + 2 * GLA_V_COLS + GLA_GATE_RANK + MERGE_COLS

kernel_name = 'hybrid_nsa_gla_gated_merge_block'


def rmsnorm(x, gain):
    xf = x.astype(jnp.float32)
    y = xf * lax.rsqrt(jnp.mean(xf * xf, axis=-1, keepdims=True) + EPS)
    return (y * gain.astype(jnp.float32)).astype(x.dtype)


def partial_rope(x, pos):
    half = ROPE_DIM // 2
    inv_freq = ROPE_THETA ** (-jnp.arange(half, dtype=jnp.float32) / half)
    ang = pos.astype(jnp.float32)[:, None] * inv_freq[None, :]
    cos = jnp.cos(ang).astype(x.dtype)
    sin = jnp.sin(ang).astype(x.dtype)
    x1 = x[..., :half]
    x2 = x[..., half:ROPE_DIM]
    return jnp.concatenate([x1 * cos - x2 * sin, x2 * cos + x1 * sin, x[..., ROPE_DIM:]], axis=-1)


def masked_softmax(scores, mask):
    p = jax.nn.softmax(jnp.where(mask, scores, NEG_INF), axis=-1)
    return p * mask


def compress_blocks(kv, pe, w1, w2):
    b, g, t, d = kv.shape
    ratio = CMP_LEN // CMP_STRIDE
    n_sub = t // CMP_STRIDE
    n_cmp = n_sub - ratio + 1
    sub = kv.reshape(b, g, n_sub, CMP_STRIDE, d)
    blocks = jnp.concatenate([sub[:, :, r:r + n_cmp] for r in range(ratio)], axis=3)
    blocks = (blocks + pe).reshape(b, g, n_cmp, CMP_LEN * d)
    return jax.nn.silu(blocks @ w1) @ w2


def selection_overlap(n_cmp, n_slc):
    c_start = np.arange(n_cmp)[:, None] * CMP_STRIDE
    s_start = np.arange(n_slc)[None, :] * SLC_LEN
    ov = (c_start < s_start + SLC_LEN) & (c_start + CMP_LEN > s_start)
    return jnp.asarray(ov, dtype=jnp.float32)


def nsa_attention(q, k_cmp, v_cmp, k_slc, v_slc, k_win, v_win, gates):
    b, g, hg, t_len, d = q.shape
    pos = jnp.arange(t_len)
    scale = NSA_HEAD_DIM ** -0.5
    q_rot = partial_rope(q, pos)
    k_slc = partial_rope(k_slc, pos)
    k_win = partial_rope(k_win, pos)
    n_cmp = k_cmp.shape[2]
    cmp_end = jnp.arange(n_cmp) * CMP_STRIDE + CMP_LEN - 1
    n_slc = t_len // SLC_LEN
    topk = min(SLC_TOPK, n_slc)
    overlap = selection_overlap(n_cmp, n_slc)
    kb = k_slc.reshape(b, g, n_slc, SLC_LEN, d)
    vb = v_slc.reshape(b, g, n_slc, SLC_LEN, d)
    kw = jnp.pad(k_win, ((0, 0), (0, 0), (WINDOW, 0), (0, 0)))
    vw = jnp.pad(v_win, ((0, 0), (0, 0), (WINDOW, 0), (0, 0)))
    bi = jnp.arange(b)[:, None, None, None]
    gi = jnp.arange(g)[None, :, None, None]
    blk_ids = jnp.arange(n_slc)

    def query_block(i):
        s = i * Q_BLOCK
        t = s + jnp.arange(Q_BLOCK)
        q_raw = lax.dynamic_slice_in_dim(q, s, Q_BLOCK, axis=3) * scale
        q_pos = lax.dynamic_slice_in_dim(q_rot, s, Q_BLOCK, axis=3) * scale
        sc = jnp.einsum('bghqd,bgnd->bghqn', q_raw, k_cmp, preferred_element_type=jnp.float32)
        p_c = masked_softmax(sc, cmp_end[None, :] <= t[:, None])
        o_c = jnp.einsum('bghqn,bgnd->bghqd', p_c.astype(v_cmp.dtype), v_cmp)
        imp = jnp.einsum('bgqn,nj->bgqj', p_c.sum(axis=2), overlap)
        cur = t[:, None] // SLC_LEN
        valid = blk_ids[None, :] * SLC_LEN <= t[:, None]
        forced = (blk_ids[None, :] == 0) | (blk_ids[None, :] == cur) | (blk_ids[None, :] == cur - 1)
        score = jnp.where(valid, jnp.where(forced, FORCE_SCORE, imp), -jnp.inf)
        top_v, idx = lax.top_k(score, topk)
        ks = kb[bi, gi, idx]
        vs = vb[bi, gi, idx].reshape(b, g, Q_BLOCK, topk * SLC_LEN, d)
        ss = jnp.einsum('bghqd,bgqnld->bghqnl', q_pos, ks, preferred_element_type=jnp.float32)
        ss = ss.reshape(b, g, hg, Q_BLOCK, topk * SLC_LEN)
        tok = idx[..., None] * SLC_LEN + jnp.arange(SLC_LEN)
        m_s = (jnp.isfinite(top_v)[..., None] & (tok <= t[None, None, :, None, None]))
        m_s = m_s.reshape(b, g, 1, Q_BLOCK, topk * SLC_LEN)
        p_s = masked_softmax(ss, m_s)
        o_s = jnp.einsum('bghqm,bgqmd->bghqd', p_s.astype(vs.dtype), vs)
        kwin = lax.dynamic_slice_in_dim(kw, s, WINDOW + Q_BLOCK, axis=2)
        vwin = lax.dynamic_slice_in_dim(vw, s, WINDOW + Q_BLOCK, axis=2)
        kpos = s - WINDOW + jnp.arange(WINDOW + Q_BLOCK)
        dist = t[:, None] - kpos[None, :]
        m_w = (kpos[None, :] >= 0) & (dist >= 0) & (dist < WINDOW)
        sw = jnp.einsum('bghqd,bgkd->bghqk', q_pos, kwin, preferred_element_type=jnp.float32)
        p_w = masked_softmax(sw, m_w)
        o_w = jnp.einsum('bghqk,bgkd->bghqd', p_w.astype(vwin.dtype), vwin)
        gt = lax.dynamic_slice_in_dim(gates, s, Q_BLOCK, axis=3)
        return gt[..., 0:1] * o_c + gt[..., 1:2] * o_s + gt[..., 2:3] * o_w

    out = lax.map(query_block, jnp.arange(t_len // Q_BLOCK))
    return out.transpose(1, 0, 4, 2, 3, 5).reshape(b, t_len, g * hg * d)


def gla_chunked(q, k, v, log_a):
    b, h, t_len, dk = q.shape
    dv = v.shape[-1]
    n_chunk = t_len // GLA_CHUNK

    def to_chunks(a):
        return jnp.moveaxis(a.astype(jnp.float32).reshape(b, h, n_chunk, GLA_CHUNK, a.shape[-1]), 2, 0)

    causal = jnp.tril(jnp.ones((GLA_CHUNK, GLA_CHUNK), dtype=bool))

    def step(state, inp):
        qc, kc, vc, gc = inp
        cum = jnp.cumsum(gc, axis=2)
        last = cum[:, :, -1:, :]
        inter = jnp.einsum('bhtd,bhdv->bhtv', qc * jnp.exp(cum), state)
        diff = jnp.where(causal[:, :, None], cum[:, :, :, None, :] - cum[:, :, None, :, :], -jnp.inf)
        attn = jnp.einsum('bhtd,bhsd,bhtsd->bhts', qc, kc, jnp.exp(diff))
        out = inter + jnp.einsum('bhts,bhsv->bhtv', attn, vc)
        state = jnp.exp(last)[:, :, 0, :, None] * state + jnp.einsum('bhsd,bhsv->bhdv', kc * jnp.exp(last - cum), vc)
        return state, out

    s0 = jnp.zeros((b, h, dk, dv), jnp.float32)
    _, o = lax.scan(step, s0, (to_chunks(q), to_chunks(k), to_chunks(v), to_chunks(log_a)))
    return jnp.moveaxis(o, 0, 2).reshape(b, h, t_len, dv)


def split_in_proj(proj):
    widths = [NSA_Q_COLS, NSA_KV_COLS, NSA_GATE_COLS, GLA_QK_COLS, GLA_QK_COLS, GLA_V_COLS,
              GLA_GATE_RANK, GLA_V_COLS, MERGE_COLS]
    cuts = [int(c) for c in np.cumsum(widths)[:-1]]
    return jnp.split(proj, cuts, axis=-1)


def setup_inputs(seed: int = 0) -> dict:
    key = jax.random.key(seed)
    ks = jax.random.split(key, 20)
    L = DEPTH

    def nrm(k, shape, scale):
        return jax.random.normal(k, shape, jnp.float32) * scale

    flat_blk = CMP_LEN * NSA_HEAD_DIM
    return {
        'x': nrm(ks[0], (BATCH, SEQ, D_MODEL), 1.0),
        'norm_mix': 1.0 + nrm(ks[1], (L, D_MODEL), 0.02),
        'w_in': nrm(ks[2], (L, D_MODEL, IN_COLS), D_MODEL ** -0.5),
        'cmp_pe_k': nrm(ks[3], (L, CMP_LEN, NSA_HEAD_DIM), 0.1),
        'cmp_pe_v': nrm(ks[4], (L, CMP_LEN, NSA_HEAD_DIM), 0.1),
        'cmp_k_w1': nrm(ks[5], (L, flat_blk, CMP_HIDDEN), flat_blk ** -0.5),
        'cmp_k_w2': nrm(ks[6], (L, CMP_HIDDEN, NSA_HEAD_DIM), CMP_HIDDEN ** -0.5),
        'cmp_v_w1': nrm(ks[7], (L, flat_blk, CMP_HIDDEN), flat_blk ** -0.5),
        'cmp_v_w2': nrm(ks[8], (L, CMP_HIDDEN, NSA_HEAD_DIM), CMP_HIDDEN ** -0.5),
        'gla_gate_w2': nrm(ks[9], (L, GLA_GATE_RANK, GLA_QK_COLS), GLA_GATE_RANK ** -0.5),
        'gla_gate_b': nrm(ks[10], (L, GLA_QK_COLS), 0.1),
        'gla_norm': 1.0 + nrm(ks[11], (L, GLA_DV), 0.02),
        'w_up_nsa': nrm(ks[12], (L, NSA_Q_COLS, D_MODEL), NSA_Q_COLS ** -0.5),
        'w_up_gla': nrm(ks[13], (L, GLA_V_COLS, D_MODEL), GLA_V_COLS ** -0.5),
        'w_out': nrm(ks[14], (L, D_MODEL, D_MODEL), D_MODEL ** -0.5),
        'norm_ffn': 1.0 + nrm(ks[15], (L, D_MODEL), 0.02),
        'w_ffn_gate': nrm(ks[16], (L, D_MODEL, D_FF), D_MODEL ** -0.5),
        'w_ffn_up': nrm(ks[17], (L, D_MODEL, D_FF), D_MODEL ** -0.5),
        'w_ffn_down': nrm(ks[18], (L, D_FF, D_MODEL), D_FF ** -0.5),
        'norm_final': 1.0 + nrm(ks[19], (D_MODEL,), 0.02),
    }


def reference(x, norm_mix, w_in, cmp_pe_k, cmp_pe_v, cmp_k_w1, cmp_k_w2, cmp_v_w1, cmp_v_w2,
              gla_gate_w2, gla_gate_b, gla_norm, w_up_nsa, w_up_gla, w_out, norm_ffn,
              w_ffn_gate, w_ffn_up, w_ffn_down, norm_final):
    b, t_len, _ = x.shape
    for layer in range(DEPTH):
        h = rmsnorm(x, norm_mix[layer])
        proj = h @ w_in[layer]
        q_nsa, kv_nsa, g_nsa, q_gla, k_gla, v_gla, lr_gla, r_gla, g_merge = split_in_proj(proj)
        q_a = q_nsa.reshape(b, t_len, NSA_KV_GROUPS, NSA_HPG, NSA_HEAD_DIM).transpose(0, 2, 3, 1, 4)
        kv = kv_nsa.reshape(b, t_len, 2 * N_NSA_BRANCH, NSA_KV_GROUPS, NSA_HEAD_DIM).transpose(2, 0, 3, 1, 4)
        k_cmp = compress_blocks(kv[0], cmp_pe_k[layer], cmp_k_w1[layer], cmp_k_w2[layer])
        v_cmp = compress_blocks(kv[1], cmp_pe_v[layer], cmp_v_w1[layer], cmp_v_w2[layer])
        gates_a = jax.nn.sigmoid(g_nsa).reshape(b, t_len, NSA_KV_GROUPS, NSA_HPG, N_NSA_BRANCH).transpose(0, 2, 3, 1, 4)
        o_a = nsa_attention(q_a, k_cmp, v_cmp, kv[2], kv[3], kv[4], kv[5], gates_a)
        log_a = jax.nn.log_sigmoid((lr_gla @ gla_gate_w2[layer] + gla_gate_b[layer]).astype(jnp.float32)) / GLA_GATE_TAU
        heads = lambda a, dh: a.reshape(b, t_len, GLA_HEADS, dh).transpose(0, 2, 1, 3)
        o_b = gla_chunked(heads(q_gla * GLA_DK ** -0.5, GLA_DK), heads(k_gla, GLA_DK),
                          heads(v_gla, GLA_DV), heads(log_a, GLA_DK))
        o_b = rmsnorm(o_b.transpose(0, 2, 1, 3).astype(x.dtype), gla_norm[layer])
        o_b = (o_b * jax.nn.silu(r_gla.reshape(b, t_len, GLA_HEADS, GLA_DV))).reshape(b, t_len, GLA_V_COLS)
        gate_a, gate_b = jnp.split(jax.nn.sigmoid(g_merge), 2, axis=-1)
        merged = gate_a * (o_a @ w_up_nsa[layer]) + gate_b * (o_b @ w_up_gla[layer])
        x = x + merged @ w_out[layer]
        h = rmsnorm(x, norm_ffn[layer])
        x = x + (jax.nn.silu(h @ w_ffn_gate[layer]) * (h @ w_ffn_up[layer])) @ w_ffn_down[layer]
    return rmsnorm(x, norm_final)
```

```python
import numpy as np
import ml_dtypes
import concourse.bass as bass
import concourse.mybir as mybir
from concourse.bass_utils import run_bass_kernel_spmd

F32 = mybir.dt.float32
BF16 = mybir.dt.bfloat16
AF = mybir.ActivationFunctionType
ALU = mybir.AluOpType
NPBF = ml_dtypes.bfloat16

DEBUG = False
import os as _os
K_NB = int(_os.environ.get('K_NB', '16'))
K_NSA = int(_os.environ.get('K_NSA', '1'))
K_TAIL = int(_os.environ.get('K_TAIL', '1'))
K_GLA = int(_os.environ.get('K_GLA', '1'))
K_CORES = int(_os.environ.get('K_CORES', '8'))
K_PIPE = int(_os.environ.get('K_PIPE', '1'))
K_B = int(_os.environ.get('K_B', '9'))
K_X = int(_os.environ.get('K_X', '0'))
K_MR = int(_os.environ.get('K_MR', '20'))
K_SPLIT = int(_os.environ.get('K_SPLIT', '0'))
K_STAGE = float(_os.environ.get('K_STAGE', '99'))


class _Stop(Exception):
    pass


def _stage(n):
    if K_STAGE <= n:
        raise _Stop()
_dbg = {}

SEQ = 8192
DM = 1024
NT = 64
NB = 16
NEG = -30000.0
EPS = 1e-6
DFF = 2816
NFF = 22

C_Q, C_KV, C_GN, C_QG, C_KG, C_VG, C_LR, C_RG, C_GM = 0, 512, 1280, 1304, 1560, 1816, 2328, 2344, 2856


class Buf:
    __slots__ = ("w", "r", "excl")

    def __init__(self):
        self.w = None
        self.r = []
        self.excl = False


class TB:
    def __init__(self, t, n=1):
        self.t = t
        self.bs = [Buf() for _ in range(n)]

    @property
    def b(self):
        return self.bs[0]


class _Rec:
    def __init__(self):
        self.call = None

    def __getattr__(self, name):
        def f(*a, **kw):
            self.call = (name, a, kw)
            return self
        return f


def _record(fn):
    if fn is None:
        return None
    r = _Rec()
    fn(r)
    return r.call


class Prog:
    ENGS = ("tensor", "vector", "scalar", "gpsimd", "sync")

    def __init__(self, nc, stack):
        self.nc = nc
        self.stack = stack
        self.semh = {}
        for e in self.ENGS:
            self.semh["E:" + e] = stack.enter_context(nc.semaphore("c_" + e))
        self.cnt = {e: 0 for e in self.ENGS}
        self.q = {e: [] for e in self.ENGS}
        self.waited = {e: {} for e in self.ENGS}
        self.streams = {}
        self.cap = None

    def replay(self, item):
        kind, eng, call, reads, writes, stream = item
        if kind == "op":
            self._op(eng, call, reads, writes)
        else:
            self._dma(eng, call, reads, writes, stream)

    def _collect(self, eng, reads, writes, extra=()):
        need = {}

        def add(d):
            if d is None:
                return
            k, v, src = d
            if src == "tensor" and eng == "tensor":
                return
            if need.get(k, 0) < v:
                need[k] = v

        for b in reads:
            add(b.w)
            if b.excl:
                for r in b.r:
                    if r[2] != eng:
                        add(r)
        for b in writes:
            add(b.w)
            for r in b.r:
                add(r)
        for d in extra:
            add(d)
        waits = []
        wd = self.waited[eng]
        for k, v in need.items():
            if wd.get(k, 0) >= v:
                continue
            wd[k] = v
            waits.append((k, v))
        return waits

    def op(self, eng, fn, reads=(), writes=()):
        if self.cap is not None:
            self.cap.append(("op", eng, _record(fn), list(reads), list(writes), None))
            return
        self._op(eng, _record(fn), reads, writes)

    def _op(self, eng, call, reads, writes):
        waits = self._collect(eng, reads, writes)
        self.cnt[eng] += 1
        me = ("E:" + eng, self.cnt[eng], eng)
        self.q[eng].append((waits, call, "E:" + eng, 1))
        for b in reads:
            b.r.append(me)
        for b in writes:
            b.w = me
            b.r = []

    def dma(self, queue, fn, reads=(), writes=(), stream="s"):
        if self.cap is not None:
            self.cap.append(("dma", queue, _record(fn), list(reads), list(writes), stream))
            return
        self._dma(queue, _record(fn), reads, writes, stream)

    def _dma(self, queue, call, reads, writes, stream):
        key = "D:" + stream
        if key not in self.semh:
            self.semh[key] = self.stack.enter_context(self.nc.semaphore("d_" + stream))
            self.streams[key] = 0
        extra = []
        if self.streams[key] > 0:
            extra.append((key, 16 * self.streams[key], "dma"))
        waits = self._collect(queue, reads, writes, extra)
        self.streams[key] += 1
        me = (key, 16 * self.streams[key], "dma")
        self.q[queue].append((waits, call, key, 16))
        for b in reads:
            b.r.append(me)
        for b in writes:
            b.w = me
            b.r = []

    def drain(self, queue, bufs):
        waits = self._collect(queue, bufs, bufs)
        self.q[queue].append((waits, None, None, 0))

    def emit(self, block):
        semh = self.semh
        for eng in self.ENGS:
            items = self.q[eng]
            self.q[eng] = []
            if not items:
                continue

            def body(e, items=items):
                for waits, fn, k, inc in items:
                    for wk, wv in waits:
                        e.wait_ge(semh[wk], wv)
                    if fn is not None:
                        name, a, kw = fn
                        getattr(e, name)(*a, **kw).then_inc(semh[k], inc)

            getattr(block, eng)(body)


def bcast(ap, pos, n):
    dims = [list(d) for d in ap.ap]
    dims.insert(1 + pos, [0, n])
    return bass.AP(ap.tensor, ap.offset, dims)


def v3(ap, a):
    return ap.rearrange("p (a b) -> p a b", a=a)


def build_nc():
    from contextlib import ExitStack

    nc = bass.Bass("TRN2", target_bir_lowering=False)

    def din(name, shape, dt=F32):
        return nc.dram_tensor(name, list(shape), dt, kind="ExternalInput").ap()

    xl = din("xl", [SEQ, DM])
    w_fm = din("w_fm", [DM, 784])
    w_tm = din("w_tm", [DM, 1024])
    w_fo = din("w_fo", [DM, 1536])
    w_to = din("w_to", [DM, 536])
    gmixT = din("gmixT", [128, 8])
    gffnT = din("gffnT", [128, 8])
    gfin = din("gfin", [128, DM])
    ggla = din("ggla", [128, 128])
    w1k = din("w1k", [128, 32 * 128])
    w1v = din("w1v", [128, 32 * 128])
    w2k = din("w2k", [128, 128])
    w2v = din("w2v", [128, 64])
    pekT = din("pekT", [128, 32])
    pevT = din("pevT", [128, 32])
    w2aug = din("w2aug", [17, 256])
    ident_d = din("ident", [128, 128], BF16)
    identf_d = din("identf", [128, 128])
    caus_d = din("caus", [128, 128], BF16)
    winlo_d = din("winlo", [128, 128], BF16)
    trile_d = din("trile", [128, 128])
    su_d = din("su", [128, 128])
    trimask_d = din("trimask", [128, 128], BF16)
    negcol_d = din("negcol", [128, 1])
    ov_d = din("ov", [128, 4 * 128], BF16)
    epat_d = din("epat", [64, SEQ], BF16)
    ropeC_d = din("ropeC", [NB, 128, 512])
    ropeS_d = din("ropeS", [NB, 128, 512])
    valid_d = din("validtab", [128, NB * 128], BF16)
    bigf_d = din("bigf", [128, NB * 128], BF16)
    cmpA_d = din("cmpA", [128, NB * 128], BF16)
    cmpZ0_d = din("cmpZ0", [128, 128], BF16)
    winj0_d = din("winj0", [128, 3 * 128], BF16)
    w_gm = din("w_gm", [16, 128, 1024])
    w_up = din("w_up", [16, 128, 512])
    w_out = din("w_out", [8, 128, 1024])
    w_g = din("w_g", [NFF, 128, 1024])
    w_u = din("w_u", [NFF, 128, 1024])
    w_d = din("w_d", [NFF, 128, 1024])
    y = nc.dram_tensor("y", [NB * 128, DM], F32, kind="ExternalOutput").ap()
    if DEBUG:
        oab = nc.dram_tensor("oab", [NB * 128, 1024], BF16, kind="ExternalOutput").ap()
    else:
        oab = nc.dram_tensor("oab", [NB * 128, 1024], BF16).ap()

    with ExitStack() as top:
        P = Prog(nc, top)

        def sb(name, shape, dt, n=1, st=top):
            return TB(st.enter_context(nc.sbuf_tensor("s_" + name, list(shape), dt)), n)

        def ps(name, shape, dt, st=top):
            tb = TB(st.enter_context(nc.psum_tensor("p_" + name, list(shape), dt)))
            tb.b.excl = True
            return tb

        TP = ps("TP", [128, 1024], BF16)
        GB = [ps(f"G{i}", [128, 512], F32) for i in range(3)]
        OC = [ps(f"OC{i}", [128, 512], F32) for i in range(2)]
        OS = ps("OS", [128, 512], F32)
        OW = ps("OW", [128, 512], F32)
        gstate = {"i": 0, "n": 2}

        def gbank():
            gstate["i"] = (gstate["i"] + 1) % gstate["n"]
            return GB[gstate["i"]]

        ident = sb("ident", [128, 128], BF16)
        gmix = sb("gmix", [128, 8], F32)
        gffn = sb("gffn", [128, 8], F32)
        XT = [sb(f"xt{i}", [128, DM], F32) for i in range(2)]
        hb = sb("hb", [128, DM], BF16)
        junk = hb
        hb2 = sb("hb2", [128, DM], BF16)
        hbs = [hb, hb2]
        st4 = sb("st4", [128, 10], F32, 5)
        nhalf = sb("nhalf", [128, 4], F32)
        P.op("gpsimd", lambda e: e.memset(nhalf.t[:], -0.5), writes=[nhalf.b])
        xslot = {"i": 0}

        def load(queue, dst, src, stream, tb=None, bidx=0):
            P.dma(queue, lambda e: e.dma_start(out=dst, in_=src), writes=[(tb.bs[bidx])], stream=stream)

        load("sync", ident.t[:], ident_d, "c0", ident)
        load("sync", gmix.t[:], gmixT, "c1", gmix)
        load("sync", gffn.t[:], gffnT, "c2", gffn)

        def rstd_from(src_ap, src_b, n, slot=0, jk=None):
            scol = 2 * slot
            sbf = st4.bs[slot]
            jk = jk or junk
            P.op("scalar", lambda e: e.activation(out=jk.t[:, 0:n], in_=src_ap, func=AF.Square,
                                                  accum_out=st4.t[:, scol:scol + 1]),
                 reads=[src_b], writes=[jk.b, sbf])
            P.op("vector", lambda e: e.tensor_scalar(out=st4.t[:, scol:scol + 1], in0=st4.t[:, scol:scol + 1],
                                                     scalar1=1.0 / n, scalar2=EPS, op0=ALU.mult, op1=ALU.add),
                 reads=[sbf], writes=[sbf])
            P.op("gpsimd", lambda e: e.tensor_tensor(out=st4.t[:, scol + 1:scol + 2], in0=st4.t[:, scol:scol + 1],
                                                     in1=nhalf.t[:, 0:1], op=ALU.pow),
                 reads=[sbf, nhalf.b], writes=[sbf])

        def norm_chain(src_ap, src_b, slot):
            h = hbs[slot % 2]
            rstd_from(src_ap, src_b, DM, slot, h)
            P.op("vector", lambda e: e.tensor_scalar(out=h.t[:], in0=src_ap, scalar1=st4.t[:, 2 * slot + 1:2 * slot + 2],
                                                     scalar2=None, op0=ALU.mult), reads=[src_b, st4.bs[slot]], writes=[h.b])

        def norm_tr(slot, gain, dst3, dst_b):
            h = hbs[slot % 2]
            for kc in range(8):
                P.op("tensor", lambda e, kc=kc: e.transpose(out=TP.t[:, kc * 128:(kc + 1) * 128],
                                                            in_=h.t[:, kc * 128:(kc + 1) * 128], identity=ident.t[:]),
                     reads=[h.b, ident.b], writes=[TP.b])
            P.op("vector", lambda e: e.tensor_tensor(out=dst3, in0=v3(TP.t[:], 8), in1=bcast(gain.t[:, 0:8], 1, 128),
                                                     op=ALU.mult), reads=[TP.b, gain.b], writes=[dst_b])

        def norm_T(src_ap, src_b, gain, dst3, dst_b):
            norm_chain(src_ap, src_b, 0)
            norm_tr(0, gain, dst3, dst_b)

        with ExitStack() as ms:
            def msb(name, shape, dt, n=1):
                return sb(name, shape, dt, n, ms)

            wfm = msb("wfm", [128, 8, 784], BF16)
            wtm = msb("wtm", [128, 8, 1024], BF16)
            wfo = msb("wfo", [128, 8, 1536], BF16)
            wto = msb("wto", [128, 8, 536], BF16)
            KT = [msb(f"KT{g}", [128, SEQ], BF16, NB) for g in range(2)]
            VS = msb("VS", [128, NT, 2, 65], BF16, NB)
            KWT = [msb(f"KWT{g}", [128, 12 * 128], BF16, 3) for g in range(2)]
            VW = msb("VW", [128, 12, 2, 65], BF16, 3)
            KC = [msb(f"KC{g}", [128, 512], BF16, NB) for g in range(2)]
            VC = [msb(f"VC{g}", [128, 4, 193], BF16, NB) for g in range(2)]
            CS = [[msb(f"CS{i}{g}", [128, 16, 33], BF16) for g in range(2)] for i in range(2)]
            W1 = [msb(f"W1{i}", [128, 32, 128], BF16) for i in range(2)]
            W2 = [msb("W20", [128, 128], BF16), msb("W21", [128, 64], BF16)]
            PE_T = [msb(f"peT{i}", [128, 32], BF16) for i in range(2)]
            C1 = msb("c1", [128, 2], F32)
            hT = msb("hT", [128, 8, 512], BF16)
            RC = msb("ropeC", [128, 512], F32)
            RS = msb("ropeS", [128, 512], F32)
            rt1 = msb("rt1", [128, 512], F32)
            rt2 = msb("rt2", [128, 512], F32)
            caus = msb("caus", [128, 128], BF16)
            winlo = msb("winlo", [128, 128], BF16)
            trile = msb("trile", [128, 128], F32)
            su = msb("su", [128, 128], F32)
            trimask = msb("trimask", [128, 128], BF16)
            negcol = msb("negcol", [128, 1], F32)
            validt = [msb(f"validt{i}", [128, 1, 128], BF16) for i in range(2)]
            bigf = [msb(f"bigf{i}", [128, 1, 128], BF16) for i in range(2)]
            cmpA = [msb(f"cmpA{i}", [128, 1, 128], BF16) for i in range(2)]
            cmpZ0 = msb("cmpZ0", [128, 128], BF16)
            winj0 = msb("winj0", [128, 3, 128], BF16)
            w2a = msb("w2a", [128, 256], F32)
            gglat = msb("gglat", [128, 128], F32)
            lrT = msb("lrT", [128, 512], F32)
            ge = msb("ge", [128, 256], F32)
            gl = msb("gl", [128, 256], F32)
            gE2 = ge
            gel = msb("gel", [128, 2], F32)
            kd2 = msb("kd2", [128, 256], BF16)
            vtm = msb("vtm", [128, 512], BF16)
            Sst = msb("Sst", [128, 2, 128], F32)
            Sbf = msb("Sbf", [128, 2, 128], BF16)
            gEc = msb("gEc", [128, 2, 128], BF16)
            gEn = msb("gEn", [128, 2, 128], BF16)
            qdTz = msb("qdTz", [128, 2, 2, 128], BF16)
            kdT = msb("kdT", [128, 2, 128], BF16)
            Am = msb("Am", [128, 4, 128], BF16)
            gss = msb("gss", [128, 8], F32)
            gsr = msb("gsr", [128, 512], BF16)
            ob = msb("ob", [128, 512], BF16)
            oa = msb("oa", [128, 512], BF16)
            hidT = msb("hidT", [128, 4, 32], BF16)
            vcst = msb("vcst", [32, 2, 64], BF16)
            qraw = [msb(f"qraw{i}", [128, 4, 128], BF16) for i in range(2)]
            qrot = [msb(f"qrot{i}", [128, 4, 128], BF16) for i in range(2)]
            QA = [[msb(f"QA{g}{v}", [128, 4, 128], BF16) for v in range(2)] for g in range(2)]
            kf = msb("kf", [128, 256], BF16)
            accc = msb("accc", [128, 4, 64], BF16)
            identf = msb("identf", [128, 128], F32)
            PT = [msb(f"PT{i}", [128, 4, 128], BF16) for i in range(2)]
            pstate = {"i": 0}
            imp = msb("imp", [128, 128], F32)
            sc = imp
            osb = msb("osb", [128, 2, 260], BF16)
            wk = msb("wk", [128, 128], F32)
            t16 = msb("t16", [128, 16], F32)
            B2 = msb("B2", [128, 256], F32)
            zt = msb("zt", [128, 12], F32)
            gts = [msb(f"gts{i}", [128, 24], F32) for i in range(2)]
            coef = msb("coef", [128, 12], F32)
            acc = msb("acc", [128, 64], F32)
            onecol = msb("onecol", [128, 1], F32)
            P.op("gpsimd", lambda e: e.memset(onecol.t[:], 1.0), writes=[onecol.b])

            def wload(dst, src, cols, stream):
                P.dma("gpsimd", lambda e: e.dma_start(out=dst.t[:], in_=src.rearrange("(k p) n -> p k n", p=128)),
                      writes=[dst.b], stream=stream)

            P.dma("gpsimd", lambda e: e.dma_start(out=W1[0].t[:], in_=w1k.rearrange("p (r h) -> p r h", r=32)),
                  writes=[W1[0].b], stream="w4")
            P.dma("gpsimd", lambda e: e.dma_start(out=W1[1].t[:], in_=w1v.rearrange("p (r h) -> p r h", r=32)),
                  writes=[W1[1].b], stream="w5")
            P.dma("gpsimd", lambda e: e.dma_start(out=W2[0].t[:], in_=w2k), writes=[W2[0].b], stream="w6")
            P.dma("gpsimd", lambda e: e.dma_start(out=W2[1].t[:], in_=w2v), writes=[W2[1].b], stream="w7")
            P.dma("gpsimd", lambda e: e.dma_start(out=PE_T[0].t[:], in_=pekT), writes=[PE_T[0].b], stream="w8")
            P.dma("gpsimd", lambda e: e.dma_start(out=PE_T[1].t[:], in_=pevT), writes=[PE_T[1].b], stream="w9")
            wload(wfm, w_fm, 784, "w0")
            wload(wtm, w_tm, 1024, "w1")
            wload(wto, w_to, 536, "w3")
            wload(wfo, w_fo, 1536, "w2")
            for (tb_, src_, nm) in ((caus, caus_d, "c3"), (winlo, winlo_d, "c4"), (trile, trile_d, "c5"),
                                    (su, su_d, "c6"), (trimask, trimask_d, "c7"), (negcol, negcol_d, "c8"),
                                    (cmpZ0, cmpZ0_d, "c9"), (gglat, ggla, "c10"), (identf, identf_d, "c19")):
                load("gpsimd", tb_.t[:], src_, nm, tb_)
            load("gpsimd", winj0.t[:], v3(winj0_d, 3), "c14", winj0)
            P.op("gpsimd", lambda e: e.memset(w2a.t[:], 0.0), writes=[w2a.b])
            load("gpsimd", w2a.t[0:17, :], w2aug, "c15", w2a)
            for g in range(2):
                P.op("gpsimd", lambda e, g=g: e.memset(VC[g].t[:], 0.0), writes=VC[g].bs)
                P.op("gpsimd", lambda e, g=g: e.memset(VC[g].t[:, :, 64:65], 1.0), writes=VC[g].bs)
                P.dma("gpsimd", lambda e, g=g: e.dma_start(out=VC[g].t[:, :, 65:193], in_=v3(ov_d, 4)),
                      writes=VC[g].bs, stream=f"c16{g}")
            for g in range(2):
                P.op("gpsimd", lambda e, g=g: e.memset(KC[g].t[:], 0.0), writes=KC[g].bs)
                P.op("gpsimd", lambda e, g=g: e.memset(KWT[g].t[:], 0.0), writes=KWT[g].bs)
            P.op("gpsimd", lambda e: e.memset(qdTz.t[:], 0.0), writes=[qdTz.b])
            P.op("gpsimd", lambda e: e.memset(VS.t[:, :, :, 64:65], 1.0), writes=VS.bs)
            P.op("gpsimd", lambda e: e.memset(VW.t[:, :, :, 64:65], 1.0), writes=VW.bs)
            for ty_ in range(2):
                for g_ in range(2):
                    P.op("gpsimd", lambda e, ty_=ty_, g_=g_: e.memset(CS[ty_][g_].t[:], 0.0), writes=[CS[ty_][g_].b])
            P.op("gpsimd", lambda e: e.memset(lrT.t[:], 0.0), writes=[lrT.b])
            P.op("gpsimd", lambda e: e.memset(lrT.t[0:32, :], 1.0), writes=[lrT.b])
            P.op("gpsimd", lambda e: e.memset(Sst.t[:], 0.0), writes=[Sst.b])
            P.dma("gpsimd", lambda e: e.dma_start(out=KT[0].t[64:128, :], in_=epat_d), writes=KT[0].bs, stream="c17")
            P.dma("gpsimd", lambda e: e.dma_start(out=KT[1].t[0:64, :], in_=epat_d), writes=KT[1].bs, stream="c18")
            def emit_c1():
                for ty in range(2):
                    g_ = gbank()
                    for r in range(32):
                        P.op("tensor", lambda e, ty=ty, r=r, g_=g_: e.matmul(
                            g_.t[:, 0:1], lhsT=W1[ty].t[0:64, r, :], rhs=PE_T[ty].t[0:64, r:r + 1],
                            start=(r == 0), stop=(r == 31)), reads=[W1[ty].b, PE_T[ty].b], writes=[g_.b])
                    P.op("vector", lambda e, ty=ty, g_=g_: e.tensor_copy(out=C1.t[:, ty:ty + 1], in_=g_.t[:, 0:1]),
                         reads=[g_.b], writes=[C1.b])

            def nextP():
                pstate["i"] = (pstate["i"] + 1) % 2
                return PT[pstate["i"]]

            def attention_branch(kv_list, Oacc_list, ncol, hpb, gb):
                nkv = len(kv_list)

                def qk(idx):
                    kl, klb, qr, qrb, mk, mkb, va, vab = kv_list[idx]
                    S = gb()
                    P.op("tensor", lambda e: e.matmul(v3(S.t[:], 4), lhsT=kl, rhs=qr, start=True, stop=(mk is None)),
                         reads=klb + qrb, writes=[S.b])
                    if mk is not None:
                        P.op("tensor", lambda e: e.matmul(v3(S.t[:], 4), lhsT=ident.t[:], rhs=bcast(mk, 0, 4),
                                                          start=False, stop=True), reads=[ident.b] + mkb, writes=[S.b])
                    return S

                Sl = [None] * (nkv + 2)
                Sl[0] = qk(0)
                if nkv > 1:
                    Sl[1] = qk(1)
                for idx in range(nkv):
                    S = Sl[idx]
                    va, vab = kv_list[idx][6], kv_list[idx][7]
                    Pt = nextP()
                    P.op("scalar", lambda e: e.activation(out=Pt.t[:], in_=v3(S.t[:], 4), func=AF.Exp, scale=0.125),
                         reads=[S.b], writes=[Pt.b])
                    if idx + 2 < nkv:
                        Sl[idx + 2] = qk(idx + 2)
                    for h in range(4):
                        O = Oacc_list[h // hpb]
                        c0 = (h % hpb) * ncol
                        P.op("tensor", lambda e: e.matmul(
                            O.t[:, c0:c0 + ncol], lhsT=Pt.t[:, h, :], rhs=va,
                            start=(idx == 0 and h % hpb == 0), stop=(idx == nkv - 1), skip_group_check=True),
                            reads=[Pt.b] + vab, writes=[O.b])
                    yield

            SBK = [OC[0], OC[1], GB[2]]
            ACC = [OS, OW]
            bst = {"i": 0}

            def gbankB():
                bst["i"] = (bst["i"] + 1) % 3
                return SBK[bst["i"]]

            marks = {}

            def partA(j):
                par = j % 2
                Tl = 4 * j + 3
                P.dma("sync", lambda e: e.dma_start(out=RC.t[:], in_=ropeC_d[j]), writes=[RC.b], stream="rc")
                P.dma("sync", lambda e: e.dma_start(out=RS.t[:], in_=ropeS_d[j]), writes=[RS.b], stream="rs")
                P.dma("sync", lambda e: e.dma_start(out=validt[par].t[:, 0, :], in_=valid_d[:, j * 128:(j + 1) * 128]),
                      writes=[validt[par].b], stream=f"tv{par}")
                P.dma("sync", lambda e: e.dma_start(out=bigf[par].t[:, 0, :], in_=bigf_d[:, j * 128:(j + 1) * 128]),
                      writes=[bigf[par].b], stream=f"tb{par}")
                P.dma("sync", lambda e: e.dma_start(out=cmpA[par].t[:, 0, :], in_=cmpA_d[:, j * 128:(j + 1) * 128]),
                      writes=[cmpA[par].b], stream=f"ta{par}")
                def chain(i):
                    l = 4 * j + i
                    xs = xslot["i"] = (xslot["i"] + 1) % 2
                    xt = XT[xs]
                    P.dma("sync", lambda e: e.dma_start(out=xt.t[:], in_=xl[l * 128:(l + 1) * 128, :]),
                          writes=[xt.b], stream=f"x{xs}")
                    norm_chain(xt.t[:], xt.b, i)

                def trn(i):
                    norm_tr(i, gmix, hT.t[:, :, i * 128:(i + 1) * 128], hT.b)

                chain(0)
                chain(1)
                trn(0)
                chain(2)
                trn(1)
                chain(3)
                trn(2)
                trn(3)
                yield

                _stage(1)
                def fm_proj(m, M=128):
                    g_ = gbank()
                    for kc in range(8):
                        P.op("tensor", lambda e: e.matmul(
                            g_.t[:, :], lhsT=(wfm.t[:, kc, 0:128] if m == 6 else wfm.t[:, kc, 16 + m * 128:16 + (m + 1) * 128]), rhs=hT.t[:, kc, :],
                            start=(kc == 0), stop=(kc == 7)), reads=[wfm.b, hT.b], writes=[g_.b])
                    return g_

                for ty in range(2):
                    if j > 0:
                        for g in range(2):
                            P.op("gpsimd", lambda e: e.tensor_copy(out=CS[ty][g].t[:, :, 0:1], in_=CS[ty][g].t[:, :, 32:33]),
                                 reads=[CS[ty][g].b], writes=[CS[ty][g].b])
                    g_ = fm_proj(ty)
                    for g in range(2):
                        rw = slice(64 * g, 64 * g + 64)
                        P.op("scalar", lambda e: e.copy(out=CS[ty][g].t[rw, :, 1:33],
                                                        in_=g_.t[rw, 0:512].rearrange("p (n r) -> p r n", r=16)),
                             reads=[g_.b], writes=[CS[ty][g].b])
                    yield

                def rope_mul(ga, gb_):
                    P.op("vector", lambda e: e.tensor_tensor(out=rt1.t[:], in0=ga.t[:], in1=RC.t[:], op=ALU.mult),
                         reads=[ga.b, RC.b], writes=[rt1.b])
                    P.op("vector", lambda e: e.tensor_tensor(out=rt2.t[:], in0=gb_.t[:], in1=RS.t[:], op=ALU.mult),
                         reads=[gb_.b, RS.b], writes=[rt2.b])

                ga = fm_proj(2)
                gb_ = fm_proj(3)
                rope_mul(ga, gb_)
                for g in range(2):
                    rw = slice(64 * g, 64 * g + 64)
                    P.op("gpsimd", lambda e: e.tensor_tensor(out=KT[g].t[rw, j * 512:(j + 1) * 512],
                                                             in0=rt1.t[rw, :], in1=rt2.t[rw, :], op=ALU.add),
                         reads=[rt1.b, rt2.b], writes=[KT[g].bs[j]])
                yield
                ga = fm_proj(4)
                gb_ = fm_proj(5)
                rope_mul(ga, gb_)
                ring0 = (j % 3) * 512
                for g in range(2):
                    rw = slice(64 * g, 64 * g + 64)
                    P.op("gpsimd", lambda e: e.tensor_tensor(out=KWT[g].t[rw, ring0:ring0 + 512], in0=rt1.t[rw, :],
                                                             in1=rt2.t[rw, :], op=ALU.add),
                         reads=[rt1.b, rt2.b], writes=[KWT[g].bs[j % 3]])
                yield
                g_ = fm_proj(6, 16)
                P.op("vector", lambda e: e.tensor_copy(out=lrT.t[0:16, :], in_=g_.t[0:16, :]), reads=[g_.b], writes=[lrT.b])
                yield

                marks[1] = len(P.cap) if P.cap is not None else 0
                _stage(2)
                for i in range(4):
                    l = 4 * j + i
                    own = (i == 3)
                    tok = slice(i * 128, (i + 1) * 128)
                    ga = gbank()
                    gb_ = gbank()
                    for bnk, g_ in ((0, ga), (1, gb_)):
                        for kc in range(8):
                            P.op("tensor", lambda e: e.matmul(
                                g_.t[:], lhsT=hT.t[:, kc, tok], rhs=wtm.t[:, kc, bnk * 512:(bnk + 1) * 512],
                                start=(kc == 0), stop=(kc == 7)), reads=[hT.b, wtm.b], writes=[g_.b])
                    P.op("vector", lambda e: e.tensor_copy(out=VS.t[:, l, :, 0:64], in_=v3(ga.t[:, 0:128], 2)),
                         reads=[ga.b], writes=[VS.bs[j]])
                    if K_X != 2:
                        P.op("vector", lambda e: e.tensor_copy(out=VW.t[:, l % 12, :, 0:64], in_=v3(ga.t[:, 128:256], 2)),
                             reads=[ga.b], writes=[VW.bs[j % 3]])
                    if K_X != 1:
                        P.op("vector", lambda e: e.tensor_copy(out=kf.t[:], in_=ga.t[:, 256:512]), reads=[ga.b], writes=[kf.b])
                    P.op("vector", lambda e: e.tensor_copy(out=vtm.t[:], in_=gb_.t[:]), reads=[gb_.b], writes=[vtm.b])
                    gz = gbank()
                    P.op("tensor", lambda e: e.matmul(gz.t[:, 0:256], lhsT=lrT.t[:, tok], rhs=w2a.t[:, :],
                                                      start=True, stop=True), reads=[lrT.b, w2a.b], writes=[gz.b])
                    P.op("scalar", lambda e: e.activation(out=ge.t[:], in_=gz.t[:, 0:256], func=AF.Exp, scale=-1.0),
                         reads=[gz.b], writes=[ge.b])
                    P.op("scalar", lambda e: e.activation(out=gl.t[:], in_=ge.t[:], func=AF.Ln, bias=onecol.t[:, 0:1]),
                         reads=[ge.b, onecol.b], writes=[gl.b])
                    _stage(2.1)
                    yield
                    gr = gbank()
                    P.op("tensor", lambda e: e.matmul(gr.t[:, 0:256], lhsT=su.t[:], rhs=gl.t[:], start=True, stop=True),
                         reads=[su.b, gl.b], writes=[gr.b])
                    for p2 in range(2):
                        P.op("tensor", lambda e: e.matmul(
                            gr.t[:, 256 + p2:257 + p2], lhsT=gl.t[:, p2 * 128:(p2 + 1) * 128], rhs=negcol.t[:],
                            start=True, stop=True), reads=[gl.b, negcol.b], writes=[gr.b])
                    P.op("scalar", lambda e: e.activation(out=gE2.t[:], in_=gr.t[:, 0:256], func=AF.Exp),
                         reads=[gr.b], writes=[gE2.b])
                    P.op("scalar", lambda e: e.activation(out=gel.t[:], in_=gr.t[:, 256:258], func=AF.Exp),
                         reads=[gr.b], writes=[gel.b])
                    P.op("vector", lambda e: e.tensor_tensor(out=kd2.t[:], in0=kf.t[:], in1=gE2.t[:], op=ALU.mult),
                         reads=[kf.b, gE2.b], writes=[kd2.b])
                    if own:
                        P.op("scalar", lambda e: e.copy(out=Sbf.t[:], in_=Sst.t[:]), reads=[Sst.b], writes=[Sbf.b])
                    _stage(2.2)
                    gs = gbank()
                    for h in range(4):
                        p2, e2 = divmod(h, 2)
                        P.op("tensor", lambda e: e.matmul(
                            gs.t[:, h * 128:(h + 1) * 128], lhsT=kd2.t[:, 128 * p2:128 * p2 + 128],
                            rhs=vtm.t[:, 128 * h:128 * h + 128], start=True, stop=True),
                            reads=[kd2.b, vtm.b], writes=[gs.b])
                    for h in range(4):
                        p2, e2 = divmod(h, 2)
                        rw = slice(64 * e2, 64 * e2 + 64)
                        P.op("vector", lambda e: e.scalar_tensor_tensor(
                            out=Sst.t[rw, p2, :], in0=Sst.t[rw, p2, :], scalar=gel.t[rw, p2:p2 + 1],
                            in1=gs.t[rw, h * 128:(h + 1) * 128], op0=ALU.mult, op1=ALU.add),
                            reads=[Sst.b, gel.b, gs.b], writes=[Sst.b])
                    yield

                marks[2] = len(P.cap) if P.cap is not None else 0
                _stage(3)
                if j == 0:
                    emit_c1()
                nn0 = 1 if j == 0 else 0
                cnt = 32 - nn0
                n_start = 32 * j - 1 + nn0
                ghs = [gbank(), gbank()]
                for ty in range(2):
                    for g in range(2):
                        cidx = ty * 2 + g
                        gh = ghs[g]
                        for r in range(32):
                            c0 = nn0 + (1 if r >= 16 else 0)
                            P.op("tensor", lambda e: e.matmul(
                                gh.t[:, cidx * 32:cidx * 32 + cnt], lhsT=W1[ty].t[:, r, :],
                                rhs=CS[ty][g].t[:, r % 16, c0:c0 + cnt],
                                start=(r == 0), stop=(r == 31)), reads=[W1[ty].b, CS[ty][g].b], writes=[gh.b])
                        P.op("scalar", lambda e: e.activation(
                            out=hidT.t[:, cidx, 0:cnt], in_=gh.t[:, cidx * 32:cidx * 32 + cnt], func=AF.Silu,
                            bias=C1.t[:, ty:ty + 1]), reads=[gh.b, C1.b], writes=[hidT.b])
                        yield
                gk = gbank()
                for g in range(2):
                    P.op("tensor", lambda e: e.matmul(gk.t[:, 256 + 32 * g:256 + 32 * g + cnt], lhsT=W2[0].t[:],
                                                      rhs=hidT.t[:, g, 0:cnt], start=True, stop=True),
                         reads=[W2[0].b, hidT.b], writes=[gk.b])
                for g in range(2):
                    P.op("vector", lambda e: e.tensor_copy(out=KC[g].t[64 * g:64 * g + 64, n_start:n_start + cnt],
                                                           in_=gk.t[64 * g:64 * g + 64, 256 + 32 * g:256 + 32 * g + cnt]),
                         reads=[gk.b], writes=[KC[g].bs[j]])
                for g in range(2):
                    P.op("tensor", lambda e: e.matmul(gk.t[0:cnt, 64 + 64 * g:128 + 64 * g], lhsT=hidT.t[:, 2 + g, 0:cnt],
                                                      rhs=W2[1].t[:], start=True, stop=True),
                         reads=[W2[1].b, hidT.b], writes=[gk.b])
                P.op("vector", lambda e: e.tensor_copy(out=vcst.t[0:cnt, :, :], in_=v3(gk.t[0:cnt, 64:192], 2)),
                     reads=[gk.b], writes=[vcst.b])
                for g in range(2):
                    n = n_start
                    while n < n_start + cnt:
                        m, p0 = divmod(n, 128)
                        ln = min(128 - p0, n_start + cnt - n)
                        s0 = n - n_start
                        P.dma("gpsimd", lambda e: e.dma_start(
                            out=VC[g].t[p0:p0 + ln, m, 0:64], in_=vcst.t[s0:s0 + ln, g, :]),
                            reads=[vcst.b], writes=[VC[g].bs[j]], stream=f"vc{g}")
                        n += ln
                yield

                marks[3] = len(P.cap) if P.cap is not None else 0
                _stage(4)
                otok = slice(384, 512)
                gq = gbank()
                gqs = gbank()
                for (g_, c0) in ((gq, 0), (gqs, 512)):
                    for m in range(4):
                        for kc in range(8):
                            P.op("tensor", lambda e: e.matmul(
                                g_.t[:, m * 128:(m + 1) * 128], lhsT=wfo.t[:, kc, c0 + m * 128:c0 + (m + 1) * 128],
                                rhs=hT.t[:, kc, otok], start=(kc == 0), stop=(kc == 7)),
                                reads=[wfo.b, hT.b], writes=[g_.b])
                _stage(4.1)
                P.op("vector", lambda e: e.tensor_copy(out=qraw[par].t[:], in_=v3(gq.t[:], 4)), reads=[gq.b], writes=[qraw[par].b])
                crope = bcast(RC.t[:, 384:512], 0, 4)
                srope = bcast(RS.t[:, 384:512], 0, 4)
                P.op("vector", lambda e: e.tensor_tensor(out=v3(rt1.t[:], 4), in0=v3(gq.t[:], 4), in1=crope, op=ALU.mult),
                     reads=[gq.b, RC.b], writes=[rt1.b])
                P.op("vector", lambda e: e.tensor_tensor(out=v3(rt2.t[:], 4), in0=v3(gqs.t[:], 4), in1=srope, op=ALU.mult),
                     reads=[gqs.b, RS.b], writes=[rt2.b])
                P.op("gpsimd", lambda e: e.tensor_tensor(out=qrot[par].t[:], in0=v3(rt1.t[:], 4), in1=v3(rt2.t[:], 4), op=ALU.add),
                     reads=[rt1.b, rt2.b], writes=[qrot[par].b])
                _stage(4.2)
                yield
                gt = gbank()
                for kc in range(8):
                    P.op("tensor", lambda e: e.matmul(gt.t[:, 0:24], lhsT=hT.t[:, kc, otok], rhs=wto.t[:, kc, 0:24],
                                                      start=(kc == 0), stop=(kc == 7)),
                         reads=[hT.b, wto.b], writes=[gt.b])
                P.op("scalar", lambda e: e.activation(out=gts[par].t[:], in_=gt.t[:, 0:24], func=AF.Sigmoid),
                     reads=[gt.b], writes=[gts[par].b])

                _stage(5)
                gc = gbank()
                for p2 in range(2):
                    P.op("tensor", lambda e: e.matmul(gc.t[:, p2 * 128:(p2 + 1) * 128],
                                                      lhsT=gl.t[:, p2 * 128:(p2 + 1) * 128], rhs=trile.t[:],
                                                      start=True, stop=True), reads=[gl.b, trile.b], writes=[gc.b])
                P.op("scalar", lambda e: e.activation(out=gEc.t[:], in_=v3(gc.t[:, 0:256], 2), func=AF.Exp),
                     reads=[gc.b], writes=[gEc.b])
                P.op("scalar", lambda e: e.activation(out=gEn.t[:], in_=v3(gc.t[:, 0:256], 2), func=AF.Exp, scale=-1.0),
                     reads=[gc.b], writes=[gEn.b])
                gg = gbank()
                for m in range(4):
                    for kc in range(8):
                        P.op("tensor", lambda e: e.matmul(
                            gg.t[:, m * 128:(m + 1) * 128], lhsT=wfo.t[:, kc, 1024 + m * 128:1024 + (m + 1) * 128],
                            rhs=hT.t[:, kc, otok], start=(kc == 0), stop=(kc == 7)),
                            reads=[wfo.b, hT.b], writes=[gg.b])
                for e2 in range(2):
                    rws = slice(64 * e2, 64 * e2 + 64)
                    P.op("vector", lambda e: e.scalar_tensor_tensor(
                        out=qdTz.t[rws, :, e2, :], in0=v3(gg.t[rws, 0:256], 2), scalar=0.125, in1=gEc.t[rws, :, :],
                        op0=ALU.mult, op1=ALU.mult), reads=[gg.b, gEc.b], writes=[qdTz.b])
                P.op("vector", lambda e: e.tensor_tensor(out=kdT.t[:], in0=v3(gg.t[:, 256:512], 2), in1=gEn.t[:], op=ALU.mult),
                     reads=[gg.b, gEn.b], writes=[kdT.b])
                yield
                ga2 = gbank()
                for h in range(4):
                    p2, e2 = divmod(h, 2)
                    P.op("tensor", lambda e: e.matmul(
                        ga2.t[:, h * 128:(h + 1) * 128], lhsT=kdT.t[:, p2, :],
                        rhs=qdTz.t[:, p2, e2, :], start=True, stop=True),
                        reads=[kdT.b, qdTz.b], writes=[ga2.b])
                P.op("vector", lambda e: e.tensor_tensor(out=Am.t[:], in0=v3(ga2.t[:], 4), in1=bcast(trimask.t[:], 0, 4),
                                                         op=ALU.mult), reads=[ga2.b, trimask.b], writes=[Am.b])
                go = gbank()
                for h in range(4):
                    p2, e2 = divmod(h, 2)
                    P.op("tensor", lambda e: e.matmul(
                        go.t[:, h * 128:(h + 1) * 128], lhsT=qdTz.t[:, p2, e2, :],
                        rhs=Sbf.t[:, p2, :], start=True, stop=False, skip_group_check=True),
                        reads=[qdTz.b, Sbf.b], writes=[go.b])
                    P.op("tensor", lambda e: e.matmul(
                        go.t[:, h * 128:(h + 1) * 128], lhsT=Am.t[:, h, :], rhs=vtm.t[:, 128 * h:128 * h + 128],
                        start=False, stop=True, skip_group_check=True), reads=[Am.b, vtm.b], writes=[go.b])
                gr2 = gbank()
                for kc in range(8):
                    P.op("tensor", lambda e: e.matmul(gr2.t[:], lhsT=hT.t[:, kc, otok], rhs=wto.t[:, kc, 24:536],
                                                      start=(kc == 0), stop=(kc == 7)),
                         reads=[hT.b, wto.b], writes=[gr2.b])
                P.op("scalar", lambda e: e.activation(out=gsr.t[:], in_=gr2.t[:], func=AF.Silu), reads=[gr2.b], writes=[gsr.b])
                P.op("vector", lambda e: e.tensor_tensor(out=v3(gsr.t[:], 4), in0=v3(gsr.t[:], 4), in1=bcast(gglat.t[:], 0, 4),
                                                         op=ALU.mult), reads=[gsr.b, gglat.b], writes=[gsr.b])
                yield
                for h in range(4):
                    P.op("scalar", lambda e: e.activation(out=junk.t[:, 0:128], in_=go.t[:, h * 128:(h + 1) * 128],
                                                          func=AF.Square, accum_out=gss.t[:, h:h + 1]),
                         reads=[go.b], writes=[junk.b, gss.b])
                P.op("vector", lambda e: e.tensor_scalar(out=gss.t[:, 0:4], in0=gss.t[:, 0:4], scalar1=1.0 / 128, scalar2=EPS,
                                                         op0=ALU.mult, op1=ALU.add), reads=[gss.b], writes=[gss.b])
                P.op("gpsimd", lambda e: e.tensor_tensor(out=gss.t[:, 4:8], in0=gss.t[:, 0:4], in1=nhalf.t[:, 0:4], op=ALU.pow),
                     reads=[gss.b, nhalf.b], writes=[gss.b])
                for h in range(4):
                    P.op("vector", lambda e: e.scalar_tensor_tensor(
                        out=ob.t[:, h * 128:(h + 1) * 128], in0=go.t[:, h * 128:(h + 1) * 128], scalar=gss.t[:, 4 + h:5 + h],
                        in1=gsr.t[:, h * 128:(h + 1) * 128], op0=ALU.mult, op1=ALU.mult),
                        reads=[go.b, gss.b, gsr.b], writes=[ob.b])
                P.dma("sync", lambda e: e.dma_start(out=oab[j * 128:(j + 1) * 128, 512:1024], in_=ob.t[:]),
                      reads=[ob.b], stream="ob")
                yield

            def partB(j):
                par = j % 2
                Tl = 4 * j + 3
                ntl = (8 * Tl + 6) // 128 + 1
                qr_, qo_, gt_ = qraw[par], qrot[par], gts[par]
                vt_, bf_, ca_ = validt[par], bigf[par], cmpA[par]
                for g in range(2):
                    rows = slice(64 * g, 64 * g + 64)
                    arow = slice(64, 128) if g == 0 else slice(0, 64)
                    kv = []
                    for m in range(ntl):
                        mk, mkb = None, []
                        if m == ntl - 1:
                            mk, mkb = ca_.t[:, 0, :], [ca_.b]
                        elif m == 0:
                            mk, mkb = cmpZ0.t[:], [cmpZ0.b]
                        kv.append((KC[g].t[:, m * 128:(m + 1) * 128], KC[g].bs[0:j + 1], qr_.t[:], [qr_.b], mk, mkb,
                                   VC[g].t[:, m, :], VC[g].bs[0:j + 1]))
                    yield from attention_branch(kv, ACC, 193, 2, gbankB)
                    for h in range(4):
                        O = ACC[h // 2]
                        c0 = (h % 2) * 193
                        P.op("vector", lambda e: e.tensor_scalar(
                            out=zt.t[:, h:h + 1], in0=O.t[:, c0 + 64:c0 + 65], scalar1=1e-30, scalar2=None, op0=ALU.add),
                            reads=[O.b], writes=[zt.b])
                    P.op("vector", lambda e: e.reciprocal(out=zt.t[:, 0:4], in_=zt.t[:, 0:4]), reads=[zt.b], writes=[zt.b])
                    for h in range(4):
                        O = ACC[h // 2]
                        c0 = (h % 2) * 193
                        if h == 0:
                            P.op("vector", lambda e: e.tensor_scalar(
                                out=imp.t[:], in0=O.t[:, c0 + 65:c0 + 193], scalar1=zt.t[:, 0:1], scalar2=None, op0=ALU.mult),
                                reads=[O.b, zt.b], writes=[imp.b])
                        else:
                            P.op("vector", lambda e: e.scalar_tensor_tensor(
                                out=imp.t[:], in0=O.t[:, c0 + 65:c0 + 193], scalar=zt.t[:, h:h + 1], in1=imp.t[:],
                                op0=ALU.mult, op1=ALU.add), reads=[O.b, zt.b, imp.b], writes=[imp.b])
                    P.op("vector", lambda e: e.tensor_tensor(out=coef.t[:, 0:4], in0=gt_.t[:, g * 12:g * 12 + 12:3],
                                                             in1=zt.t[:, 0:4], op=ALU.mult),
                         reads=[gt_.b, zt.b], writes=[coef.b])
                    for h in range(4):
                        O = ACC[h // 2]
                        c0 = (h % 2) * 193
                        P.op("vector", lambda e: e.tensor_scalar(
                            out=accc.t[:, h, :], in0=O.t[:, c0:c0 + 64], scalar1=coef.t[:, h:h + 1], scalar2=None, op0=ALU.mult),
                            reads=[O.b, coef.b], writes=[accc.b])
                    yield
                    P.op("vector", lambda e: e.tensor_tensor(out=sc.t[:], in0=imp.t[:], in1=bf_.t[:, 0, :], op=ALU.max),
                         reads=[imp.b, bf_.b], writes=[sc.b])
                    P.op("vector", lambda e: e.tensor_tensor(out=sc.t[:], in0=sc.t[:], in1=vt_.t[:, 0, :], op=ALU.mult),
                         reads=[sc.b, vt_.b], writes=[sc.b])
                    P.op("vector", lambda e: e.max(out=t16.t[:, 0:8], in_=sc.t[:]), reads=[sc.b], writes=[t16.b])
                    P.op("vector", lambda e: e.match_replace(out=wk.t[:], in_to_replace=t16.t[:, 0:8], in_values=sc.t[:],
                                                             imm_value=-1e9), reads=[sc.b, t16.b], writes=[wk.b])
                    P.op("vector", lambda e: e.max(out=t16.t[:, 8:16], in_=wk.t[:]), reads=[wk.b], writes=[t16.b])
                    P.op("vector", lambda e: e.scalar_tensor_tensor(out=wk.t[:], in0=sc.t[:], scalar=t16.t[:, 15:16],
                                                                    in1=vt_.t[:, 0, :], op0=ALU.is_ge, op1=ALU.mult),
                         reads=[sc.b, t16.b, vt_.b], writes=[wk.b])
                    for hh in range(2):
                        P.op("vector", lambda e: e.tensor_scalar(
                            out=B2.t[:, hh * 128:(hh + 1) * 128], in0=wk.t[:], scalar1=-NEG, scalar2=NEG,
                            op0=ALU.mult, op1=ALU.add), reads=[wk.b], writes=[B2.b])
                    yield
                    kv = []
                    for l in range(max(0, Tl - 4), Tl + 1):
                        mk, mkb = None, []
                        if j == 0 and l < 3:
                            mk, mkb = winj0.t[:, l, :], [winj0.b]
                        elif l == Tl:
                            mk, mkb = caus.t[:], [caus.b]
                        elif l == Tl - 4:
                            mk, mkb = winlo.t[:], [winlo.b]
                        rp = (l % 12) * 128
                        kv.append((KWT[g].t[:, rp:rp + 128], [KWT[g].bs[(l // 4) % 3]], qo_.t[:], [qo_.b], mk, mkb,
                                   VW.t[:, l % 12, g, :], [VW.bs[(l // 4) % 3]]))
                    yield from attention_branch(kv, [ACC[0]], 65, 4, gbankB)
                    Tb = gbankB()
                    P.op("tensor", lambda e: e.transpose(out=Tb.t[:, 0:128], in_=B2.t[:, 0:128], identity=identf.t[:]),
                         reads=[B2.b, identf.b], writes=[Tb.b])
                    P.op("tensor", lambda e: e.transpose(out=Tb.t[:, 128:256], in_=B2.t[:, 64:192], identity=identf.t[:]),
                         reads=[B2.b, identf.b], writes=[Tb.b])
                    nver = 1 if Tl < 32 else 2
                    for ver in range(nver):
                        tsel = (1 - ver) if g == 0 else ver
                        q_ = QA[g][ver]
                        P.op("scalar", lambda e: e.copy(out=q_.t[rows, :, :], in_=qo_.t[rows, :, :]),
                             reads=[qo_.b], writes=[q_.b])
                        P.op("vector", lambda e: e.tensor_copy(
                            out=q_.t[arow, :, :], in_=bcast(Tb.t[arow, tsel * 128:(tsel + 1) * 128], 0, 4)),
                            reads=[Tb.b], writes=[q_.b])
                    yield
                    kv = []
                    for i in range(Tl + 1):
                        mk, mkb = (caus.t[:], [caus.b]) if i == Tl else (None, [])
                        q_ = QA[g][0 if i < 32 else 1]
                        kv.append((KT[g].t[:, i * 128:(i + 1) * 128], [KT[g].bs[i // 4]], q_.t[:], [q_.b], mk, mkb,
                                   VS.t[:, i, g, :], [VS.bs[i // 4]]))
                    yield from attention_branch(kv, [ACC[1]], 65, 4, gbankB)
                    for k_ in range(2):
                        P.op("vector", lambda e: e.tensor_copy(out=osb.t[:, k_, :], in_=ACC[1 - k_].t[:, 0:260]),
                             reads=[ACC[1 - k_].b], writes=[osb.b])
                    for k_ in range(2):
                        brn = 1 + k_
                        P.op("vector", lambda e: e.tensor_scalar(
                            out=v3(zt.t[:, brn * 4:brn * 4 + 4], 4), in0=v3(osb.t[:, k_, :], 4)[:, :, 64:65],
                            scalar1=1e-30, scalar2=None, op0=ALU.add), reads=[osb.b], writes=[zt.b])
                    P.op("vector", lambda e: e.reciprocal(out=zt.t[:, 4:12], in_=zt.t[:, 4:12]), reads=[zt.b], writes=[zt.b])
                    for brn in (1, 2):
                        P.op("vector", lambda e: e.tensor_tensor(
                            out=coef.t[:, brn * 4:brn * 4 + 4], in0=gt_.t[:, g * 12 + brn:g * 12 + 12:3],
                            in1=zt.t[:, brn * 4:brn * 4 + 4], op=ALU.mult), reads=[gt_.b, zt.b], writes=[coef.b])
                    for h in range(4):
                        hc = (g * 4 + h) * 64
                        P.op("vector", lambda e: e.scalar_tensor_tensor(
                            out=acc.t[:], in0=osb.t[:, 0, h * 65:h * 65 + 64], scalar=coef.t[:, 4 + h:5 + h], in1=accc.t[:, h, :],
                            op0=ALU.mult, op1=ALU.add), reads=[osb.b, coef.b, accc.b], writes=[acc.b])
                        P.op("vector", lambda e: e.scalar_tensor_tensor(
                            out=oa.t[:, hc:hc + 64], in0=osb.t[:, 1, h * 65:h * 65 + 64], scalar=coef.t[:, 8 + h:9 + h],
                            in1=acc.t[:], op0=ALU.mult, op1=ALU.add), reads=[osb.b, coef.b, acc.b], writes=[oa.b])
                    yield
                P.dma("sync", lambda e: e.dma_start(out=oab[j * 128:(j + 1) * 128, 0:512], in_=oa.t[:]),
                      reads=[oa.b], stream="oa")
                yield

            def run_all(gen):
                try:
                    for _ in gen:
                        pass
                except _Stop:
                    pass

            def interleave(ga_, na, gb_, nb):
                ia = ib = 0
                da = db = False
                while not (da and db):
                    if db or (not da and ia * nb <= ib * na):
                        try:
                            next(ga_)
                            ia += 1
                        except StopIteration:
                            da = True
                    else:
                        try:
                            next(gb_)
                            ib += 1
                        except StopIteration:
                            db = True

            def capture(gen):
                P.cap = []
                run_all(gen)
                lst = P.cap
                P.cap = None
                return lst

            def merge(la, lb):
                na, nb = len(la), len(lb)
                ia = ib = 0
                while ia < na or ib < nb:
                    if ib >= nb or (ia < na and ia * nb * K_MR <= ib * na * 10):
                        P.replay(la[ia])
                        ia += 1
                    else:
                        P.replay(lb[ib])
                        ib += 1

            merge(capture(partA(0)), [])
            for j in range(K_NB):
                lb = capture(partB(j)) if K_B else []
                la = capture(partA(j + 1)) if j + 1 < K_NB else []
                if K_PIPE and K_SPLIT and la:
                    k_ = marks[K_SPLIT]
                    merge(la[:k_], lb)
                    merge(la[k_:], [])
                elif K_PIPE:
                    merge(la, lb)
                else:
                    merge(lb, [])
                    merge(la, [])

            P.drain("sync", [oa.b, ob.b])
            with nc.Block() as block:
                P.emit(block)

        gstate["n"] = 3
        with ExitStack() as ts:
            def tsb(name, shape, dt):
                return sb(name, shape, dt, 1, ts)

            X1 = sb("X1", [128, 8, DM], F32, 8, ts)
            bufA = tsb("bufA", [128, 8, 1024], BF16)
            oT = tsb("oT", [128, 8, 1024], BF16)
            BIG = tsb("BIG", [128, NFF * 1024], BF16)
            WR = sb("WR", [128, NFF * 512], BF16, NFF, ts)
            STG = [tsb(f"stg{i}", [128, 1024], F32) for i in range(4)]
            WBF = [tsb(f"wbf{i}", [128, 1024], BF16) for i in range(6)]
            oabts = [tsb(f"oabt{i}", [128, 1024], BF16) for i in range(2)]
            tf1 = tsb("tf1", [128, 512], F32)
            tf2 = tsb("tf2", [128, 512], F32)
            sg = tsb("sg", [128, 512], BF16)
            yts = [tsb(f"yt{i}", [128, DM], F32) for i in range(2)]
            gfint = tsb("gfint", [128, DM], F32)
            load("sync", gfint.t[:], gfin, "c20", gfint)
            gm = v3(BIG.t[:, 0:16 * 1024], 16)
            act = v3(BIG.t[:], NFF)
            WO = v3(WR.t[:, 0:8 * 1024], 8)
            WD = v3(WR.t[:], NFF)
            sst = {"s": 0, "w": 0, "c": 0}

            def stream_w(src, W, dst=None, dst_b=None):
                s_ = sst["s"] = (sst["s"] + 1) % 4
                stg = STG[s_]
                P.dma("sync", lambda e: e.dma_start(out=stg.t[:, 0:W], in_=src), writes=[stg.b], stream=f"st{s_}")
                wtb = None
                if dst is None:
                    w_ = sst["w"] = (sst["w"] + 1) % 6
                    wtb = WBF[w_]
                    dst, dst_b = wtb.t[:, 0:W], wtb.b
                sst["c"] += 1
                dst_bl = dst_b if isinstance(dst_b, list) else [dst_b]
                if sst["c"] % 3 == 0:
                    P.op("vector", lambda e: e.tensor_copy(out=dst, in_=stg.t[:, 0:W]), reads=[stg.b], writes=dst_bl)
                else:
                    P.op("scalar", lambda e: e.copy(out=dst, in_=stg.t[:, 0:W]), reads=[stg.b], writes=dst_bl)
                return wtb

            def final_tile(hf_, o8):
                o = hf_ * 8 + o8
                yt = yts[o8 % 2]
                rstd_from(X1.t[:, o8, :], X1.bs[o8], DM, 4, yt)
                P.op("vector", lambda e: e.scalar_tensor_tensor(out=yt.t[:], in0=X1.t[:, o8, :], scalar=st4.t[:, 9:10],
                                                                in1=gfint.t[:], op0=ALU.mult, op1=ALU.mult),
                     reads=[X1.bs[o8], st4.bs[4], gfint.b], writes=[yt.b])
                P.dma("sync", lambda e: e.dma_start(out=y[o * 128:(o + 1) * 128, :], in_=yt.t[:]),
                      reads=[yt.b], stream=f"y{o8 % 2}")

            for hf in range(2 if K_TAIL else 0):
                def t_chain(o8):
                    l = 4 * (hf * 8 + o8) + 3
                    P.dma("sync", lambda e: e.dma_start(out=X1.t[:, o8, :], in_=xl[l * 128:(l + 1) * 128, :]),
                          writes=[X1.bs[o8]], stream=f"x{o8 % 2}")
                    norm_chain(X1.t[:, o8, :], X1.bs[o8], o8 % 4)

                def t_tr(o8):
                    o = hf * 8 + o8
                    tsl = slice(o8 * 128, (o8 + 1) * 128)
                    norm_tr(o8 % 4, gmix, bufA.t[:, :, tsl], bufA.b)
                    oabt = oabts[o8 % 2]
                    P.dma("sync", lambda e: e.dma_start(out=oabt.t[:], in_=oab[o * 128:(o + 1) * 128, :]),
                          writes=[oabt.b], stream=f"oabt{o8 % 2}")
                    for k in range(8):
                        P.op("tensor", lambda e: e.transpose(out=TP.t[:, k * 128:(k + 1) * 128],
                                                             in_=oabt.t[:, k * 128:(k + 1) * 128], identity=ident.t[:]),
                             reads=[oabt.b, ident.b], writes=[TP.b])
                    P.op("scalar", lambda e: e.copy(out=oT.t[:, :, tsl], in_=v3(TP.t[:], 8)),
                         reads=[TP.b], writes=[oT.b])

                if hf > 0:
                    final_tile(hf - 1, 0)
                    final_tile(hf - 1, 1)
                for o8 in range(8):
                    t_chain(o8)
                    if hf > 0 and o8 + 2 < 8:
                        final_tile(hf - 1, o8 + 2)
                    if o8 >= 1:
                        t_tr(o8 - 1)
                t_tr(7)
                for m in range(16):
                    w = stream_w(w_gm[m], 1024)
                    for nb in range(2):
                        nsl = slice(nb * 512, (nb + 1) * 512)
                        g_ = gbank()
                        for kc in range(8):
                            P.op("tensor", lambda e, g_=g_, w=w, kc=kc, nsl=nsl: e.matmul(
                                g_.t[:], lhsT=v3(w.t[:], 8)[:, kc, :], rhs=bufA.t[:, kc, nsl],
                                start=(kc == 0), stop=(kc == 7)), reads=[w.b, bufA.b], writes=[g_.b])
                        P.op("scalar", lambda e, g_=g_, m=m, nsl=nsl: e.activation(out=gm[:, m, nsl], in_=g_.t[:],
                                                                                  func=AF.Sigmoid),
                             reads=[g_.b], writes=[BIG.b])
                for m in range(8):
                    wa = stream_w(w_up[m], 512)
                    wb = stream_w(w_up[8 + m], 512)
                    for nb in range(2):
                        nsl = slice(nb * 512, (nb + 1) * 512)
                        gA = gbank()
                        gB = gbank()
                        for (g_, w, k0) in ((gA, wa, 0), (gB, wb, 4)):
                            for k in range(4):
                                P.op("tensor", lambda e, g_=g_, w=w, k=k, k0=k0, nsl=nsl: e.matmul(
                                    g_.t[:], lhsT=v3(w.t[:, 0:512], 4)[:, k, :], rhs=oT.t[:, k0 + k, nsl],
                                    start=(k == 0), stop=(k == 3)), reads=[w.b, oT.b], writes=[g_.b])
                        P.op("vector", lambda e, gA=gA, m=m, nsl=nsl: e.tensor_tensor(out=tf1.t[:], in0=gA.t[:],
                                                                                     in1=gm[:, m, nsl], op=ALU.mult),
                             reads=[gA.b, BIG.b], writes=[tf1.b])
                        P.op("vector", lambda e, gB=gB, m=m, nsl=nsl: e.tensor_tensor(out=tf2.t[:], in0=gB.t[:],
                                                                                     in1=gm[:, 8 + m, nsl], op=ALU.mult),
                             reads=[gB.b, BIG.b], writes=[tf2.b])
                        P.op("gpsimd", lambda e, m=m, nsl=nsl: e.tensor_tensor(out=bufA.t[:, m, nsl], in0=tf1.t[:],
                                                                              in1=tf2.t[:], op=ALU.add),
                             reads=[tf1.b, tf2.b], writes=[bufA.b])
                for kc in range(8):
                    stream_w(w_out[kc], 1024, WO[:, kc, :], [WR.bs[2 * kc], WR.bs[2 * kc + 1]])
                for o8 in range(8):
                    tsl = slice(o8 * 128, (o8 + 1) * 128)
                    for nh in range(2):
                        nsl = slice(nh * 512, (nh + 1) * 512)
                        g_ = gbank()
                        for kc in range(8):
                            P.op("tensor", lambda e, g_=g_, kc=kc, tsl=tsl, nsl=nsl: e.matmul(
                                g_.t[:], lhsT=bufA.t[:, kc, tsl], rhs=WO[:, kc, nsl], start=(kc == 0), stop=(kc == 7)),
                                reads=[bufA.b, WR.bs[2 * kc], WR.bs[2 * kc + 1]], writes=[g_.b])
                        P.op("vector", lambda e, g_=g_, o8=o8, nsl=nsl: e.tensor_tensor(
                            out=X1.t[:, o8, nsl], in0=X1.t[:, o8, nsl], in1=g_.t[:], op=ALU.add),
                            reads=[X1.bs[o8], g_.b], writes=[X1.bs[o8]])
                for o8 in range(9):
                    if o8 < 8:
                        norm_chain(X1.t[:, o8, :], X1.bs[o8], o8 % 4)
                    if o8 >= 1:
                        tsl = slice((o8 - 1) * 128, o8 * 128)
                        norm_tr((o8 - 1) % 4, gffn, bufA.t[:, :, tsl], bufA.b)
                for m in range(NFF):
                    wg = stream_w(w_g[m], 1024)
                    wu = stream_w(w_u[m], 1024)
                    for nb in range(2):
                        nsl = slice(nb * 512, (nb + 1) * 512)
                        gG = gbank()
                        gU = gbank()
                        for (g_, w) in ((gG, wg), (gU, wu)):
                            for kc in range(8):
                                P.op("tensor", lambda e, g_=g_, w=w, kc=kc, nsl=nsl: e.matmul(
                                    g_.t[:], lhsT=v3(w.t[:], 8)[:, kc, :], rhs=bufA.t[:, kc, nsl],
                                    start=(kc == 0), stop=(kc == 7)), reads=[w.b, bufA.b], writes=[g_.b])
                        P.op("scalar", lambda e, gG=gG: e.activation(out=sg.t[:], in_=gG.t[:], func=AF.Silu),
                             reads=[gG.b], writes=[sg.b])
                        P.op("vector", lambda e, gU=gU, m=m, nsl=nsl: e.tensor_tensor(out=act[:, m, nsl], in0=gU.t[:],
                                                                                     in1=sg.t[:], op=ALU.mult),
                             reads=[gU.b, sg.b], writes=[BIG.b])
                for nh in range(2):
                    nsl = slice(nh * 512, (nh + 1) * 512)
                    for m in range(NFF):
                        stream_w(w_d[m][:, nsl], 512, WD[:, m, :], WR.bs[m])
                    for o8 in range(8):
                        tsl = slice(o8 * 128, (o8 + 1) * 128)
                        g_ = gbank()
                        for m in range(NFF):
                            P.op("tensor", lambda e, g_=g_, m=m, tsl=tsl: e.matmul(
                                g_.t[:], lhsT=act[:, m, tsl], rhs=WD[:, m, :], start=(m == 0), stop=(m == NFF - 1)),
                                reads=[BIG.b, WR.bs[m]], writes=[g_.b])
                        P.op("vector", lambda e, g_=g_, o8=o8, nsl=nsl: e.tensor_tensor(
                            out=X1.t[:, o8, nsl], in0=X1.t[:, o8, nsl], in1=g_.t[:], op=ALU.add),
                            reads=[X1.bs[o8], g_.b], writes=[X1.bs[o8]])
            if K_TAIL:
                for o8 in range(8):
                    final_tile(1, o8)
            P.drain("sync", [yts[0].b, yts[1].b])
            with nc.Block() as block:
                P.emit(block)
    return nc


def _bf(a):
    return np.ascontiguousarray(a.astype(np.float32)).astype(NPBF)


def _shared_inputs(inp):
    f32 = np.float32
    w_in = np.asarray(inp["w_in"][0], dtype=f32)
    sw = np.concatenate([np.arange(8, 16), np.arange(0, 8), np.arange(16, 64)])
    sw2 = np.concatenate([sw, 64 + sw])

    def kv(i):
        return w_in[:, C_KV + i * 128:C_KV + (i + 1) * 128]

    w_fm = np.concatenate([w_in[:, C_LR:C_LR + 16], kv(0), kv(1), kv(2), kv(2)[:, sw2], kv(4), kv(4)[:, sw2]], axis=1)
    w_tm = np.concatenate([kv(3), kv(5), w_in[:, C_KG:C_KG + 256], w_in[:, C_VG:C_VG + 512]], axis=1)
    qcols = []
    for m in range(4):
        qcols.append(w_in[:, 64 * m:64 * m + 64])
        qcols.append(w_in[:, 64 * (4 + m):64 * (4 + m) + 64])
    q = np.concatenate(qcols, axis=1)
    qsw = np.concatenate([c[:, sw] for c in qcols], axis=1)
    w_fo = np.concatenate([q, qsw, w_in[:, C_QG:C_QG + 256], w_in[:, C_KG:C_KG + 256]], axis=1)
    w_to = np.concatenate([w_in[:, C_GN:C_GN + 24], w_in[:, C_RG:C_RG + 512]], axis=1)

    def colT(v):
        return np.ascontiguousarray(np.asarray(v, dtype=f32).reshape(8, 128).T)

    def w1l(w1):
        a = np.asarray(w1, dtype=f32).reshape(32, 64, 128).transpose(1, 0, 2).reshape(64, 32 * 128)
        return np.ascontiguousarray(np.concatenate([a, a], axis=0))

    def peT(pe):
        a = np.asarray(pe, dtype=f32).T
        return np.ascontiguousarray(np.concatenate([a, a], axis=0))

    p = np.arange(128)
    le = (p[:, None] <= p[None, :])
    n_all = (np.arange(4)[None, :, None] * 128 + p[:, None, None])
    blk = np.arange(128)[None, None, :]
    ov = ((16 * n_all < 64 * blk + 64) & (16 * n_all + 32 > 64 * blk)).astype(f32).reshape(128, 512)
    tl = np.arange(SEQ)
    epat = (np.arange(64)[:, None] == ((tl // 64) % 64)[None, :]).astype(f32)

    def tiles(w, nk, nm):
        return np.ascontiguousarray(np.asarray(w, dtype=f32).reshape(nk, 128, nm, 128).transpose(2, 1, 0, 3)
                                    .reshape(nm, 128, nk * 128))

    sh = {
        "w_fm": w_fm, "w_tm": w_tm, "w_fo": w_fo, "w_to": w_to,
        "gmixT": colT(inp["norm_mix"][0]), "gffnT": colT(inp["norm_ffn"][0]),
        "gfin": np.tile(np.asarray(inp["norm_final"], dtype=f32)[None, :], (128, 1)),
        "ggla": np.tile(np.asarray(inp["gla_norm"][0], dtype=f32)[None, :], (128, 1)),
        "w1k": w1l(inp["cmp_k_w1"][0]), "w1v": w1l(inp["cmp_v_w1"][0]),
        "w2k": np.concatenate([np.asarray(inp["cmp_k_w2"][0], dtype=f32)] * 2, axis=1), "w2v": np.asarray(inp["cmp_v_w2"][0], dtype=f32),
        "pekT": peT(inp["cmp_pe_k"][0]), "pevT": peT(inp["cmp_pe_v"][0]),
        "w2aug": np.concatenate([np.asarray(inp["gla_gate_w2"][0], dtype=f32),
                                 np.asarray(inp["gla_gate_b"][0], dtype=f32)[None, :]], axis=0),
        "ident": _bf(np.eye(128)),
        "identf": np.eye(128, dtype=f32),
        "caus": _bf(np.where(le, 0.0, NEG)),
        "winlo": _bf(np.where(p[:, None] > p[None, :], 0.0, NEG)),
        "trile": np.where(le, -1.0 / 16, 0.0).astype(f32),
        "su": np.where(p[:, None] > p[None, :], -1.0 / 16, 0.0).astype(f32),
        "trimask": _bf(le.astype(f32)),
        "negcol": np.full((128, 1), -1.0 / 16, dtype=f32),
        "ov": _bf(ov), "epat": _bf(epat),
        "w_gm": tiles(w_in[:, C_GM:C_GM + 2048], 8, 16),
        "w_up": np.concatenate([tiles(inp["w_up_nsa"][0], 4, 8), tiles(inp["w_up_gla"][0], 4, 8)], axis=0),
        "w_out": np.ascontiguousarray(np.asarray(inp["w_out"][0], dtype=f32).reshape(8, 128, 1024)),
        "w_g": tiles(inp["w_ffn_gate"][0], 8, NFF), "w_u": tiles(inp["w_ffn_up"][0], 8, NFF),
        "w_d": np.ascontiguousarray(np.asarray(inp["w_ffn_down"][0], dtype=f32).reshape(NFF, 128, 1024)),
    }
    return {k: np.ascontiguousarray(v) for k, v in sh.items()}


def _core_inputs(x, b, c):
    f32 = np.float32
    shf = 3 - c
    off = 128 * shf
    xl = np.zeros((SEQ, DM), dtype=f32)
    xl[off:] = x[b, :SEQ - off]
    half = 8
    inv_freq = (np.float32(500000.0) ** (-np.arange(half, dtype=f32) / np.float32(half))).astype(f32)
    pos = (np.arange(SEQ) - off).astype(f32)
    ang = (pos[:, None] * inv_freq[None, :]).astype(f32)
    cs, sn = np.cos(ang).astype(f32), np.sin(ang).astype(f32)
    C64 = np.ones((64, SEQ), dtype=f32)
    S64 = np.zeros((64, SEQ), dtype=f32)
    C64[0:8] = cs.T
    C64[8:16] = cs.T
    S64[0:8] = -sn.T
    S64[8:16] = sn.T
    C128 = np.concatenate([C64, C64], axis=0)
    S128 = np.concatenate([S64, S64], axis=0)
    ropeC = np.ascontiguousarray(C128.reshape(128, NB, 512).transpose(1, 0, 2))
    ropeS = np.ascontiguousarray(S128.reshape(128, NB, 512).transpose(1, 0, 2))
    q = np.arange(128)[:, None, None]
    j = np.arange(NB)[None, :, None]
    blk = np.arange(128)[None, None, :]
    t_loc = 128 * (4 * j + 3) + q
    cur = t_loc // 64
    valid = ((blk >= 2 * shf) & (64 * blk <= t_loc)).astype(f32)
    forced = ((blk == 2 * shf) | (blk == cur) | (blk == cur - 1))
    bigf = np.where(forced, 1.0e4, 0.0).astype(f32)
    pn = np.arange(128)[:, None, None]
    jq = np.arange(NB)[None, :, None]
    qq = np.arange(128)[None, None, :]
    Tl = 4 * jq + 3
    m_last = (8 * Tl + 6) // 128
    n = 128 * m_last + pn
    okc = (n >= 8 * shf) & (16 * n + 31 <= 128 * Tl + qq)
    cmpA = np.where(okc, 0.0, NEG).astype(f32)
    cmpZ0 = np.where((np.arange(128)[:, None] >= 8 * shf) & (np.arange(128)[None, :] >= 0), 0.0, NEG).astype(f32)
    winj0 = np.zeros((128, 3, 128), dtype=f32)
    for l in range(3):
        if l < shf:
            winj0[:, l, :] = NEG
    return {
        "xl": xl, "ropeC": ropeC, "ropeS": ropeS,
        "validtab": _bf(valid.reshape(128, NB * 128)), "bigf": _bf(bigf.reshape(128, NB * 128)),
        "cmpA": _bf(cmpA.reshape(128, NB * 128)), "cmpZ0": _bf(cmpZ0), "winj0": _bf(winj0.reshape(128, 384)),
    }


_NC_CACHE = {}


def kernel(**inputs):
    x = np.asarray(inputs["x"], dtype=np.float32)
    shared = _shared_inputs(inputs)
    in_maps = []
    for core in range(8):
        b, c = divmod(core, 4)
        m = dict(shared)
        m.update(_core_inputs(x, b, c))
        in_maps.append(m)
    if "nc" not in _NC_CACHE:
        _NC_CACHE["nc"] = build_nc()
    nc = _NC_CACHE["nc"]
    res = run_bass_kernel_spmd(nc, in_maps[:K_CORES], core_ids=list(range(K_CORES)))
    out = np.zeros((2, SEQ, DM), dtype=np.float32)
    for core in range(K_CORES):
        b, c = divmod(core, 4)
        yc = np.asarray(res.results[core]["y"], dtype=np.float32)
        for j in range(NB):
            T = 4 * j + c
            out[b, T * 128:(T + 1) * 128] = yc[j * 128:(j + 1) * 128]
        if DEBUG:
            _dbg[core] = np.asarray(res.results[core]["oab"])
    return out
```

```python
import numpy as np
import ml_dtypes
import concourse.bass as bass
import concourse.mybir as mybir
from concourse.bass_utils import run_bass_kernel_spmd

F32 = mybir.dt.float32
BF16 = mybir.dt.bfloat16
AF = mybir.ActivationFunctionType
ALU = mybir.AluOpType
NPBF = ml_dtypes.bfloat16

DEBUG = False
import os as _os
K_NB = int(_os.environ.get('K_NB', '16'))
K_NSA = int(_os.environ.get('K_NSA', '1'))
K_TAIL = int(_os.environ.get('K_TAIL', '1'))
K_GLA = int(_os.environ.get('K_GLA', '1'))
K_CORES = int(_os.environ.get('K_CORES', '8'))
K_PIPE = int(_os.environ.get('K_PIPE', '1'))
K_B = int(_os.environ.get('K_B', '9'))
K_X = int(_os.environ.get('K_X', '0'))
K_MR = int(_os.environ.get('K_MR', '20'))
K_SPLIT = int(_os.environ.get('K_SPLIT', '0'))
K_STAGE = float(_os.environ.get('K_STAGE', '99'))


class _Stop(Exception):
    pass


def _stage(n):
    if K_STAGE <= n:
        raise _Stop()
_dbg = {}

SEQ = 8192
DM = 1024
NT = 64
NB = 16
NEG = -30000.0
EPS = 1e-6
DFF = 2816
NFF = 22

C_Q, C_KV, C_GN, C_QG, C_KG, C_VG, C_LR, C_RG, C_GM = 0, 512, 1280, 1304, 1560, 1816, 2328, 2344, 2856


class Buf:
    __slots__ = ("w", "r", "excl")

    def __init__(self):
        self.w = None
        self.r = []
        self.excl = False


class TB:
    def __init__(self, t, n=1):
        self.t = t
        self.bs = [Buf() for _ in range(n)]

    @property
    def b(self):
        return self.bs[0]


class _Rec:
    def __init__(self):
        self.call = None

    def __getattr__(self, name):
        def f(*a, **kw):
            self.call = (name, a, kw)
            return self
        return f


def _record(fn):
    if fn is None:
        return None
    r = _Rec()
    fn(r)
    return r.call


class Prog:
    ENGS = ("tensor", "vector", "scalar", "gpsimd", "sync")

    def __init__(self, nc, stack):
        self.nc = nc
        self.stack = stack
        self.semh = {}
        for e in self.ENGS:
            self.semh["E:" + e] = stack.enter_context(nc.semaphore("c_" + e))
        self.cnt = {e: 0 for e in self.ENGS}
        self.q = {e: [] for e in self.ENGS}
        self.waited = {e: {} for e in self.ENGS}
        self.streams = {}
        self.cap = None

    def replay(self, item):
        kind, eng, call, reads, writes, stream = item
        if kind == "op":
            self._op(eng, call, reads, writes)
        else:
            self._dma(eng, call, reads, writes, stream)

    def _collect(self, eng, reads, writes, extra=()):
        need = {}

        def add(d):
            if d is None:
                return
            k, v, src = d
            if src == "tensor" and eng == "tensor":
                return
            if need.get(k, 0) < v:
                need[k] = v

        for b in reads:
            add(b.w)
            if b.excl:
                for r in b.r:
                    if r[2] != eng:
                        add(r)
        for b in writes:
            add(b.w)
            for r in b.r:
                add(r)
        for d in extra:
            add(d)
        waits = []
        wd = self.waited[eng]
        for k, v in need.items():
            if wd.get(k, 0) >= v:
                continue
            wd[k] = v
            waits.append((k, v))
        return waits

    def op(self, eng, fn, reads=(), writes=()):
        if self.cap is not None:
            self.cap.append(("op", eng, _record(fn), list(reads), list(writes), None))
            return
        self._op(eng, _record(fn), reads, writes)

    def _op(self, eng, call, reads, writes):
        waits = self._collect(eng, reads, writes)
        self.cnt[eng] += 1
        me = ("E:" + eng, self.cnt[eng], eng)
        self.q[eng].append((waits, call, "E:" + eng, 1))
        for b in reads:
            b.r.append(me)
        for b in writes:
            b.w = me
            b.r = []

    def dma(self, queue, fn, reads=(), writes=(), stream="s"):
        if self.cap is not None:
            self.cap.append(("dma", queue, _record(fn), list(reads), list(writes), stream))
            return
        self._dma(queue, _record(fn), reads, writes, stream)

    def _dma(self, queue, call, reads, writes, stream):
        key = "D:" + stream
        if key not in self.semh:
            self.semh[key] = self.stack.enter_context(self.nc.semaphore("d_" + stream))
            self.streams[key] = 0
        extra = []
        if self.streams[key] > 0:
            extra.append((key, 16 * self.streams[key], "dma"))
        waits = self._collect(queue, reads, writes, extra)
        self.streams[key] += 1
        me = (key, 16 * self.streams[key], "dma")
        self.q[queue].append((waits, call, key, 16))
        for b in reads:
            b.r.append(me)
        for b in writes:
            b.w = me
            b.r = []

    def drain(self, queue, bufs):
        waits = self._collect(queue, bufs, bufs)
        self.q[queue].append((waits, None, None, 0))

    def emit(self, block):
        semh = self.semh
        for eng in self.ENGS:
            items = self.q[eng]
            self.q[eng] = []
            if not items:
                continue

            def body(e, items=items):
                for waits, fn, k, inc in items:
                    for wk, wv in waits:
                        e.wait_ge(semh[wk], wv)
                    if fn is not None:
                        name, a, kw = fn
                        getattr(e, name)(*a, **kw).then_inc(semh[k], inc)

            getattr(block, eng)(body)


def bcast(ap, pos, n):
    dims = [list(d) for d in ap.ap]
    dims.insert(1 + pos, [0, n])
    return bass.AP(ap.tensor, ap.offset, dims)


def v3(ap, a):
    return ap.rearrange("p (a b) -> p a b", a=a)


def build_nc():
    from contextlib import ExitStack

    nc = bass.Bass("TRN2", target_bir_lowering=False)

    def din(name, shape, dt=F32):
        return nc.dram_tensor(name, list(shape), dt, kind="ExternalInput").ap()

    xl = din("xl", [SEQ, DM])
    w_fm = din("w_fm", [DM, 784])
    w_tm = din("w_tm", [DM, 1024])
    w_fo = din("w_fo", [DM, 1536])
    w_to = din("w_to", [DM, 536])
    gmixT = din("gmixT", [128, 8])
    gffnT = din("gffnT", [128, 8])
    gfin = din("gfin", [128, DM])
    ggla = din("ggla", [128, 128])
    w1k = din("w1k", [128, 32 * 128])
    w1v = din("w1v", [128, 32 * 128])
    w2k = din("w2k", [128, 64])
    w2v = din("w2v", [128, 64])
    pekT = din("pekT", [128, 32])
    pevT = din("pevT", [128, 32])
    w2aug = din("w2aug", [17, 256])
    ident_d = din("ident", [128, 128], BF16)
    identf_d = din("identf", [128, 128])
    caus_d = din("caus", [128, 128], BF16)
    winlo_d = din("winlo", [128, 128], BF16)
    trile_d = din("trile", [128, 128])
    su_d = din("su", [128, 128])
    trimask_d = din("trimask", [128, 128], BF16)
    negcol_d = din("negcol", [128, 1])
    ov_d = din("ov", [128, 4 * 128], BF16)
    epat_d = din("epat", [64, SEQ], BF16)
    ropeC_d = din("ropeC", [NB, 128, 512])
    ropeS_d = din("ropeS", [NB, 128, 512])
    valid_d = din("validtab", [128, NB * 128], BF16)
    bigf_d = din("bigf", [128, NB * 128], BF16)
    cmpA_d = din("cmpA", [128, NB * 128], BF16)
    cmpZ0_d = din("cmpZ0", [128, 128], BF16)
    winj0_d = din("winj0", [128, 3 * 128], BF16)
    w_gm = din("w_gm", [16, 128, 1024])
    w_up = din("w_up", [16, 128, 512])
    w_out = din("w_out", [8, 128, 1024])
    w_g = din("w_g", [NFF, 128, 1024])
    w_u = din("w_u", [NFF, 128, 1024])
    w_d = din("w_d", [NFF, 128, 1024])
    y = nc.dram_tensor("y", [NB * 128, DM], F32, kind="ExternalOutput").ap()
    if DEBUG:
        oab = nc.dram_tensor("oab", [NB * 128, 1024], BF16, kind="ExternalOutput").ap()
    else:
        oab = nc.dram_tensor("oab", [NB * 128, 1024], BF16).ap()

    with ExitStack() as top:
        P = Prog(nc, top)

        def sb(name, shape, dt, n=1, st=top):
            return TB(st.enter_context(nc.sbuf_tensor("s_" + name, list(shape), dt)), n)

        def ps(name, shape, dt, st=top):
            tb = TB(st.enter_context(nc.psum_tensor("p_" + name, list(shape), dt)))
            tb.b.excl = True
            return tb

        TP = ps("TP", [128, 1024], BF16)
        GB = [ps(f"G{i}", [128, 512], F32) for i in range(3)]
        OC = [ps(f"OC{i}", [128, 512], F32) for i in range(2)]
        OS = ps("OS", [128, 512], F32)
        OW = ps("OW", [128, 512], F32)
        gstate = {"i": 0, "n": 2}

        def gbank():
            gstate["i"] = (gstate["i"] + 1) % gstate["n"]
            return GB[gstate["i"]]

        ident = sb("ident", [128, 128], BF16)
        gmix = sb("gmix", [128, 8], F32)
        gffn = sb("gffn", [128, 8], F32)
        XT = [sb(f"xt{i}", [128, DM], F32) for i in range(2)]
        hb = sb("hb", [128, DM], BF16)
        junk = hb
        hb2 = sb("hb2", [128, DM], BF16)
        hbs = [hb, hb2]
        st4 = sb("st4", [128, 10], F32, 5)
        nhalf = sb("nhalf", [128, 4], F32)
        P.op("gpsimd", lambda e: e.memset(nhalf.t[:], -0.5), writes=[nhalf.b])
        xslot = {"i": 0}

        def load(queue, dst, src, stream, tb=None, bidx=0):
            P.dma(queue, lambda e: e.dma_start(out=dst, in_=src), writes=[(tb.bs[bidx])], stream=stream)

        load("sync", ident.t[:], ident_d, "c0", ident)
        load("sync", gmix.t[:], gmixT, "c1", gmix)
        load("sync", gffn.t[:], gffnT, "c2", gffn)

        def rstd_from(src_ap, src_b, n, slot=0, jk=None):
            scol = 2 * slot
            sbf = st4.bs[slot]
            jk = jk or junk
            P.op("scalar", lambda e: e.activation(out=jk.t[:, 0:n], in_=src_ap, func=AF.Square,
                                                  accum_out=st4.t[:, scol:scol + 1]),
                 reads=[src_b], writes=[jk.b, sbf])
            P.op("vector", lambda e: e.tensor_scalar(out=st4.t[:, scol:scol + 1], in0=st4.t[:, scol:scol + 1],
                                                     scalar1=1.0 / n, scalar2=EPS, op0=ALU.mult, op1=ALU.add),
                 reads=[sbf], writes=[sbf])
            P.op("gpsimd", lambda e: e.tensor_tensor(out=st4.t[:, scol + 1:scol + 2], in0=st4.t[:, scol:scol + 1],
                                                     in1=nhalf.t[:, 0:1], op=ALU.pow),
                 reads=[sbf, nhalf.b], writes=[sbf])

        def norm_chain(src_ap, src_b, slot):
            h = hbs[slot % 2]
            rstd_from(src_ap, src_b, DM, slot, h)
            P.op("vector", lambda e: e.tensor_scalar(out=h.t[:], in0=src_ap, scalar1=st4.t[:, 2 * slot + 1:2 * slot + 2],
                                                     scalar2=None, op0=ALU.mult), reads=[src_b, st4.bs[slot]], writes=[h.b])

        def norm_tr(slot, gain, dst3, dst_b):
            h = hbs[slot % 2]
            for kc in range(8):
                P.op("tensor", lambda e, kc=kc: e.transpose(out=TP.t[:, kc * 128:(kc + 1) * 128],
                                                            in_=h.t[:, kc * 128:(kc + 1) * 128], identity=ident.t[:]),
                     reads=[h.b, ident.b], writes=[TP.b])
            P.op("vector", lambda e: e.tensor_tensor(out=dst3, in0=v3(TP.t[:], 8), in1=bcast(gain.t[:, 0:8], 1, 128),
                                                     op=ALU.mult), reads=[TP.b, gain.b], writes=[dst_b])

        def norm_T(src_ap, src_b, gain, dst3, dst_b):
            norm_chain(src_ap, src_b, 0)
            norm_tr(0, gain, dst3, dst_b)

        with ExitStack() as ms:
            def msb(name, shape, dt, n=1):
                return sb(name, shape, dt, n, ms)

            wfm = msb("wfm", [128, 8, 784], BF16)
            wtm = msb("wtm", [128, 8, 1024], BF16)
            wfo = msb("wfo", [128, 8, 1536], BF16)
            wto = msb("wto", [128, 8, 536], BF16)
            KT = [msb(f"KT{g}", [128, SEQ], BF16, NB) for g in range(2)]
            VS = msb("VS", [128, NT, 2, 65], BF16, NB)
            KWT = [msb(f"KWT{g}", [128, 12 * 128], BF16, 3) for g in range(2)]
            VW = msb("VW", [128, 12, 2, 65], BF16, 3)
            KC = [msb(f"KC{g}", [128, 512], BF16, NB) for g in range(2)]
            VC = [msb(f"VC{g}", [128, 4, 193], BF16, NB) for g in range(2)]
            CS = [msb(f"CS{i}", [128, 16, 33], BF16) for i in range(2)]
            W1 = [msb(f"W1{i}", [128, 32, 128], BF16) for i in range(2)]
            W2 = [msb(f"W2{i}", [128, 64], BF16) for i in range(2)]
            PE_T = [msb(f"peT{i}", [128, 32], BF16) for i in range(2)]
            C1 = msb("c1", [128, 2], F32)
            hT = msb("hT", [128, 8, 512], BF16)
            RC = msb("ropeC", [128, 512], F32)
            RS = msb("ropeS", [128, 512], F32)
            rt1 = msb("rt1", [128, 512], F32)
            rt2 = msb("rt2", [128, 512], F32)
            caus = msb("caus", [128, 128], BF16)
            winlo = msb("winlo", [128, 128], BF16)
            trile = msb("trile", [128, 128], F32)
            su = msb("su", [128, 128], F32)
            trimask = msb("trimask", [128, 128], BF16)
            negcol = msb("negcol", [128, 1], F32)
            validt = [msb(f"validt{i}", [128, 1, 128], BF16) for i in range(2)]
            bigf = [msb(f"bigf{i}", [128, 1, 128], BF16) for i in range(2)]
            cmpA = [msb(f"cmpA{i}", [128, 1, 128], BF16) for i in range(2)]
            cmpZ0 = msb("cmpZ0", [128, 128], BF16)
            winj0 = msb("winj0", [128, 3, 128], BF16)
            w2a = msb("w2a", [128, 256], F32)
            gglat = msb("gglat", [128, 128], F32)
            lrT = msb("lrT", [128, 512], F32)
            ge = msb("ge", [128, 256], F32)
            gl = msb("gl", [128, 256], F32)
            gE2 = ge
            gel = msb("gel", [128, 2], F32)
            kd2 = msb("kd2", [128, 256], BF16)
            vtm = msb("vtm", [128, 512], BF16)
            Sst = msb("Sst", [128, 2, 128], F32)
            Sbf = msb("Sbf", [128, 2, 128], BF16)
            gEc = msb("gEc", [128, 2, 128], BF16)
            gEn = msb("gEn", [128, 2, 128], BF16)
            qdTz = msb("qdTz", [128, 2, 2, 128], BF16)
            kdT = msb("kdT", [128, 2, 128], BF16)
            Am = msb("Am", [128, 4, 128], BF16)
            gss = msb("gss", [128, 8], F32)
            gsr = msb("gsr", [128, 512], BF16)
            ob = msb("ob", [128, 512], BF16)
            oa = msb("oa", [128, 512], BF16)
            hidT = msb("hidT", [128, 4, 32], BF16)
            vcst = msb("vcst", [32, 2, 64], BF16)
            qraw = [msb(f"qraw{i}", [128, 4, 128], BF16) for i in range(2)]
            qrot = [msb(f"qrot{i}", [128, 4, 128], BF16) for i in range(2)]
            QA = [[msb(f"QA{g}{v}", [128, 4, 128], BF16) for v in range(2)] for g in range(2)]
            kf = msb("kf", [128, 256], F32)
            accc = msb("accc", [128, 4, 64], F32)
            identf = msb("identf", [128, 128], F32)
            PT = [msb(f"PT{i}", [128, 4, 128], BF16) for i in range(2)]
            pstate = {"i": 0}
            imp = msb("imp", [128, 128], F32)
            sc = imp
            osb = msb("osb", [128, 2, 260], F32)
            wk = msb("wk", [128, 128], F32)
            t16 = msb("t16", [128, 16], F32)
            B2 = msb("B2", [128, 256], F32)
            zt = msb("zt", [128, 12], F32)
            gts = [msb(f"gts{i}", [128, 24], F32) for i in range(2)]
            coef = msb("coef", [128, 12], F32)
            acc = msb("acc", [128, 64], F32)
            onecol = msb("onecol", [128, 1], F32)
            P.op("gpsimd", lambda e: e.memset(onecol.t[:], 1.0), writes=[onecol.b])

            def wload(dst, src, cols, stream):
                P.dma("gpsimd", lambda e: e.dma_start(out=dst.t[:], in_=src.rearrange("(k p) n -> p k n", p=128)),
                      writes=[dst.b], stream=stream)

            P.dma("gpsimd", lambda e: e.dma_start(out=W1[0].t[:], in_=w1k.rearrange("p (r h) -> p r h", r=32)),
                  writes=[W1[0].b], stream="w4")
            P.dma("gpsimd", lambda e: e.dma_start(out=W1[1].t[:], in_=w1v.rearrange("p (r h) -> p r h", r=32)),
                  writes=[W1[1].b], stream="w5")
            P.dma("gpsimd", lambda e: e.dma_start(out=W2[0].t[:], in_=w2k), writes=[W2[0].b], stream="w6")
            P.dma("gpsimd", lambda e: e.dma_start(out=W2[1].t[:], in_=w2v), writes=[W2[1].b], stream="w7")
            P.dma("gpsimd", lambda e: e.dma_start(out=PE_T[0].t[:], in_=pekT), writes=[PE_T[0].b], stream="w8")
            P.dma("gpsimd", lambda e: e.dma_start(out=PE_T[1].t[:], in_=pevT), writes=[PE_T[1].b], stream="w9")
            wload(wfm, w_fm, 784, "w0")
            wload(wtm, w_tm, 1024, "w1")
            wload(wto, w_to, 536, "w3")
            wload(wfo, w_fo, 1536, "w2")
            for (tb_, src_, nm) in ((caus, caus_d, "c3"), (winlo, winlo_d, "c4"), (trile, trile_d, "c5"),
                                    (su, su_d, "c6"), (trimask, trimask_d, "c7"), (negcol, negcol_d, "c8"),
                                    (cmpZ0, cmpZ0_d, "c9"), (gglat, ggla, "c10"), (identf, identf_d, "c19")):
                load("gpsimd", tb_.t[:], src_, nm, tb_)
            load("gpsimd", winj0.t[:], v3(winj0_d, 3), "c14", winj0)
            P.op("gpsimd", lambda e: e.memset(w2a.t[:], 0.0), writes=[w2a.b])
            load("gpsimd", w2a.t[0:17, :], w2aug, "c15", w2a)
            for g in range(2):
                P.op("gpsimd", lambda e, g=g: e.memset(VC[g].t[:], 0.0), writes=VC[g].bs)
                P.op("gpsimd", lambda e, g=g: e.memset(VC[g].t[:, :, 64:65], 1.0), writes=VC[g].bs)
                P.dma("gpsimd", lambda e, g=g: e.dma_start(out=VC[g].t[:, :, 65:193], in_=v3(ov_d, 4)),
                      writes=VC[g].bs, stream=f"c16{g}")
            for g in range(2):
                P.op("gpsimd", lambda e, g=g: e.memset(KC[g].t[:], 0.0), writes=KC[g].bs)
                P.op("gpsimd", lambda e, g=g: e.memset(KWT[g].t[:], 0.0), writes=KWT[g].bs)
            P.op("gpsimd", lambda e: e.memset(qdTz.t[:], 0.0), writes=[qdTz.b])
            P.op("gpsimd", lambda e: e.memset(VS.t[:, :, :, 64:65], 1.0), writes=VS.bs)
            P.op("gpsimd", lambda e: e.memset(VW.t[:, :, :, 64:65], 1.0), writes=VW.bs)
            P.op("gpsimd", lambda e: e.memset(CS[0].t[:], 0.0), writes=[CS[0].b])
            P.op("gpsimd", lambda e: e.memset(CS[1].t[:], 0.0), writes=[CS[1].b])
            P.op("gpsimd", lambda e: e.memset(lrT.t[:], 0.0), writes=[lrT.b])
            P.op("gpsimd", lambda e: e.memset(lrT.t[0:32, :], 1.0), writes=[lrT.b])
            P.op("gpsimd", lambda e: e.memset(Sst.t[:], 0.0), writes=[Sst.b])
            P.dma("gpsimd", lambda e: e.dma_start(out=KT[0].t[64:128, :], in_=epat_d), writes=KT[0].bs, stream="c17")
            P.dma("gpsimd", lambda e: e.dma_start(out=KT[1].t[0:64, :], in_=epat_d), writes=KT[1].bs, stream="c18")
            def emit_c1():
                for ty in range(2):
                    g_ = gbank()
                    for r in range(32):
                        P.op("tensor", lambda e, ty=ty, r=r, g_=g_: e.matmul(
                            g_.t[:, 0:1], lhsT=W1[ty].t[0:64, r, :], rhs=PE_T[ty].t[0:64, r:r + 1],
                            start=(r == 0), stop=(r == 31)), reads=[W1[ty].b, PE_T[ty].b], writes=[g_.b])
                    P.op("vector", lambda e, ty=ty, g_=g_: e.tensor_copy(out=C1.t[:, ty:ty + 1], in_=g_.t[:, 0:1]),
                         reads=[g_.b], writes=[C1.b])

            def nextP():
                pstate["i"] = (pstate["i"] + 1) % 2
                return PT[pstate["i"]]

            def attention_branch(kv_list, Oacc_list, ncol, hpb, gb):
                nkv = len(kv_list)

                def qk(idx):
                    kl, klb, qr, qrb, mk, mkb, va, vab = kv_list[idx]
                    S = gb()
                    P.op("tensor", lambda e: e.matmul(v3(S.t[:], 4), lhsT=kl, rhs=qr, start=True, stop=(mk is None)),
                         reads=klb + qrb, writes=[S.b])
                    if mk is not None:
                        P.op("tensor", lambda e: e.matmul(v3(S.t[:], 4), lhsT=ident.t[:], rhs=bcast(mk, 0, 4),
                                                          start=False, stop=True), reads=[ident.b] + mkb, writes=[S.b])
                    return S

                Sl = [None] * (nkv + 2)
                Sl[0] = qk(0)
                if nkv > 1:
                    Sl[1] = qk(1)
                for idx in range(nkv):
                    S = Sl[idx]
                    va, vab = kv_list[idx][6], kv_list[idx][7]
                    Pt = nextP()
                    P.op("scalar", lambda e: e.activation(out=Pt.t[:], in_=v3(S.t[:], 4), func=AF.Exp, scale=0.125),
                         reads=[S.b], writes=[Pt.b])
                    if idx + 2 < nkv:
                        Sl[idx + 2] = qk(idx + 2)
                    for h in range(4):
                        O = Oacc_list[h // hpb]
                        c0 = (h % hpb) * ncol
                        P.op("tensor", lambda e: e.matmul(
                            O.t[:, c0:c0 + ncol], lhsT=Pt.t[:, h, :], rhs=va,
                            start=(idx == 0 and h % hpb == 0), stop=(idx == nkv - 1), skip_group_check=True),
                            reads=[Pt.b] + vab, writes=[O.b])
                    yield

            SBK = [OC[0], OC[1], GB[2]]
            ACC = [OS, OW]
            bst = {"i": 0}

            def gbankB():
                bst["i"] = (bst["i"] + 1) % 3
                return SBK[bst["i"]]

            marks = {}

            def partA(j):
                par = j % 2
                Tl = 4 * j + 3
                P.dma("sync", lambda e: e.dma_start(out=RC.t[:], in_=ropeC_d[j]), writes=[RC.b], stream="rc")
                P.dma("sync", lambda e: e.dma_start(out=RS.t[:], in_=ropeS_d[j]), writes=[RS.b], stream="rs")
                P.dma("sync", lambda e: e.dma_start(out=validt[par].t[:, 0, :], in_=valid_d[:, j * 128:(j + 1) * 128]),
                      writes=[validt[par].b], stream=f"tv{par}")
                P.dma("sync", lambda e: e.dma_start(out=bigf[par].t[:, 0, :], in_=bigf_d[:, j * 128:(j + 1) * 128]),
                      writes=[bigf[par].b], stream=f"tb{par}")
                P.dma("sync", lambda e: e.dma_start(out=cmpA[par].t[:, 0, :], in_=cmpA_d[:, j * 128:(j + 1) * 128]),
                      writes=[cmpA[par].b], stream=f"ta{par}")
                def chain(i):
                    l = 4 * j + i
                    xs = xslot["i"] = (xslot["i"] + 1) % 2
                    xt = XT[xs]
                    P.dma("sync", lambda e: e.dma_start(out=xt.t[:], in_=xl[l * 128:(l + 1) * 128, :]),
                          writes=[xt.b], stream=f"x{xs}")
                    norm_chain(xt.t[:], xt.b, i)

                def trn(i):
                    norm_tr(i, gmix, hT.t[:, :, i * 128:(i + 1) * 128], hT.b)

                chain(0)
                chain(1)
                trn(0)
                chain(2)
                trn(1)
                chain(3)
                trn(2)
                trn(3)
                yield

                _stage(1)
                def fm_proj(m, M=128):
                    g_ = gbank()
                    for kc in range(8):
                        P.op("tensor", lambda e: e.matmul(
                            g_.t[:, :], lhsT=(wfm.t[:, kc, 0:128] if m == 6 else wfm.t[:, kc, 16 + m * 128:16 + (m + 1) * 128]), rhs=hT.t[:, kc, :],
                            start=(kc == 0), stop=(kc == 7)), reads=[wfm.b, hT.b], writes=[g_.b])
                    return g_

                for ty in range(2):
                    if j > 0:
                        P.op("gpsimd", lambda e: e.tensor_copy(out=CS[ty].t[:, :, 0:1], in_=CS[ty].t[:, :, 32:33]),
                             reads=[CS[ty].b], writes=[CS[ty].b])
                    g_ = fm_proj(ty)
                    P.op("scalar", lambda e: e.copy(out=CS[ty].t[:, :, 1:33],
                                                    in_=g_.t[:, 0:512].rearrange("p (n r) -> p r n", r=16)),
                         reads=[g_.b], writes=[CS[ty].b])
                    yield

                def rope_mul(ga, gb_):
                    P.op("vector", lambda e: e.tensor_tensor(out=rt1.t[:], in0=ga.t[:], in1=RC.t[:], op=ALU.mult),
                         reads=[ga.b, RC.b], writes=[rt1.b])
                    P.op("vector", lambda e: e.tensor_tensor(out=rt2.t[:], in0=gb_.t[:], in1=RS.t[:], op=ALU.mult),
                         reads=[gb_.b, RS.b], writes=[rt2.b])

                ga = fm_proj(2)
                gb_ = fm_proj(3)
                rope_mul(ga, gb_)
                for g in range(2):
                    rw = slice(64 * g, 64 * g + 64)
                    P.op("gpsimd", lambda e: e.tensor_tensor(out=KT[g].t[rw, j * 512:(j + 1) * 512],
                                                             in0=rt1.t[rw, :], in1=rt2.t[rw, :], op=ALU.add),
                         reads=[rt1.b, rt2.b], writes=[KT[g].bs[j]])
                yield
                ga = fm_proj(4)
                gb_ = fm_proj(5)
                rope_mul(ga, gb_)
                ring0 = (j % 3) * 512
                for g in range(2):
                    rw = slice(64 * g, 64 * g + 64)
                    P.op("gpsimd", lambda e: e.tensor_tensor(out=KWT[g].t[rw, ring0:ring0 + 512], in0=rt1.t[rw, :],
                                                             in1=rt2.t[rw, :], op=ALU.add),
                         reads=[rt1.b, rt2.b], writes=[KWT[g].bs[j % 3]])
                yield
                g_ = fm_proj(6, 16)
                P.op("vector", lambda e: e.tensor_copy(out=lrT.t[0:16, :], in_=g_.t[0:16, :]), reads=[g_.b], writes=[lrT.b])
                yield

                marks[1] = len(P.cap) if P.cap is not None else 0
                _stage(2)
                for i in range(4):
                    l = 4 * j + i
                    own = (i == 3)
                    tok = slice(i * 128, (i + 1) * 128)
                    ga = gbank()
                    gb_ = gbank()
                    for bnk, g_ in ((0, ga), (1, gb_)):
                        for kc in range(8):
                            P.op("tensor", lambda e: e.matmul(
                                g_.t[:], lhsT=hT.t[:, kc, tok], rhs=wtm.t[:, kc, bnk * 512:(bnk + 1) * 512],
                                start=(kc == 0), stop=(kc == 7)), reads=[hT.b, wtm.b], writes=[g_.b])
                    P.op("vector", lambda e: e.tensor_copy(out=VS.t[:, l, :, 0:64], in_=v3(ga.t[:, 0:128], 2)),
                         reads=[ga.b], writes=[VS.bs[j]])
                    if K_X != 2:
                        P.op("vector", lambda e: e.tensor_copy(out=VW.t[:, l % 12, :, 0:64], in_=v3(ga.t[:, 128:256], 2)),
                             reads=[ga.b], writes=[VW.bs[j % 3]])
                    if K_X != 1:
                        P.op("vector", lambda e: e.tensor_copy(out=kf.t[:], in_=ga.t[:, 256:512]), reads=[ga.b], writes=[kf.b])
                    P.op("vector", lambda e: e.tensor_copy(out=vtm.t[:], in_=gb_.t[:]), reads=[gb_.b], writes=[vtm.b])
                    gz = gbank()
                    P.op("tensor", lambda e: e.matmul(gz.t[:, 0:256], lhsT=lrT.t[:, tok], rhs=w2a.t[:, :],
                                                      start=True, stop=True), reads=[lrT.b, w2a.b], writes=[gz.b])
                    P.op("scalar", lambda e: e.activation(out=ge.t[:], in_=gz.t[:, 0:256], func=AF.Exp, scale=-1.0),
                         reads=[gz.b], writes=[ge.b])
                    P.op("scalar", lambda e: e.activation(out=gl.t[:], in_=ge.t[:], func=AF.Ln, bias=onecol.t[:, 0:1]),
                         reads=[ge.b, onecol.b], writes=[gl.b])
                    _stage(2.1)
                    yield
                    gr = gbank()
                    P.op("tensor", lambda e: e.matmul(gr.t[:, 0:256], lhsT=su.t[:], rhs=gl.t[:], start=True, stop=True),
                         reads=[su.b, gl.b], writes=[gr.b])
                    for p2 in range(2):
                        P.op("tensor", lambda e: e.matmul(
                            gr.t[:, 256 + p2:257 + p2], lhsT=gl.t[:, p2 * 128:(p2 + 1) * 128], rhs=negcol.t[:],
                            start=True, stop=True), reads=[gl.b, negcol.b], writes=[gr.b])
                    P.op("scalar", lambda e: e.activation(out=gE2.t[:], in_=gr.t[:, 0:256], func=AF.Exp),
                         reads=[gr.b], writes=[gE2.b])
                    P.op("scalar", lambda e: e.activation(out=gel.t[:], in_=gr.t[:, 256:258], func=AF.Exp),
                         reads=[gr.b], writes=[gel.b])
                    P.op("vector", lambda e: e.tensor_tensor(out=kd2.t[:], in0=kf.t[:], in1=gE2.t[:], op=ALU.mult),
                         reads=[kf.b, gE2.b], writes=[kd2.b])
                    if own:
                        P.op("scalar", lambda e: e.copy(out=Sbf.t[:], in_=Sst.t[:]), reads=[Sst.b], writes=[Sbf.b])
                    _stage(2.2)
                    gs = gbank()
                    for h in range(4):
                        p2, e2 = divmod(h, 2)
                        P.op("tensor", lambda e: e.matmul(
                            gs.t[:, h * 128:(h + 1) * 128], lhsT=kd2.t[:, 128 * p2:128 * p2 + 128],
                            rhs=vtm.t[:, 128 * h:128 * h + 128], start=True, stop=True),
                            reads=[kd2.b, vtm.b], writes=[gs.b])
                    for h in range(4):
                        p2, e2 = divmod(h, 2)
                        rw = slice(64 * e2, 64 * e2 + 64)
                        P.op("vector", lambda e: e.scalar_tensor_tensor(
                            out=Sst.t[rw, p2, :], in0=Sst.t[rw, p2, :], scalar=gel.t[rw, p2:p2 + 1],
                            in1=gs.t[rw, h * 128:(h + 1) * 128], op0=ALU.mult, op1=ALU.add),
                            reads=[Sst.b, gel.b, gs.b], writes=[Sst.b])
                    yield

                marks[2] = len(P.cap) if P.cap is not None else 0
                _stage(3)
                if j == 0:
                    emit_c1()
                nn0 = 1 if j == 0 else 0
                cnt = 32 - nn0
                n_start = 32 * j - 1 + nn0
                ghs = [gbank(), gbank()]
                for ty in range(2):
                    for g in range(2):
                        cidx = ty * 2 + g
                        gh = ghs[g]
                        for r in range(32):
                            c0 = nn0 + (1 if r >= 16 else 0)
                            P.op("tensor", lambda e: e.matmul(
                                gh.t[:, cidx * 32:cidx * 32 + cnt], lhsT=W1[ty].t[64 * g:64 * g + 64, r, :],
                                rhs=CS[ty].t[64 * g:64 * g + 64, r % 16, c0:c0 + cnt],
                                start=(r == 0), stop=(r == 31)), reads=[W1[ty].b, CS[ty].b], writes=[gh.b])
                        P.op("scalar", lambda e: e.activation(
                            out=hidT.t[:, cidx, 0:cnt], in_=gh.t[:, cidx * 32:cidx * 32 + cnt], func=AF.Silu,
                            bias=C1.t[:, ty:ty + 1]), reads=[gh.b, C1.b], writes=[hidT.b])
                        yield
                gk = gbank()
                for g in range(2):
                    P.op("tensor", lambda e: e.matmul(gk.t[64 * g:64 * g + 64, 0:cnt], lhsT=W2[0].t[:],
                                                      rhs=hidT.t[:, g, 0:cnt], start=True, stop=True),
                         reads=[W2[0].b, hidT.b], writes=[gk.b])
                for g in range(2):
                    P.op("vector", lambda e: e.tensor_copy(out=KC[g].t[64 * g:64 * g + 64, n_start:n_start + cnt],
                                                           in_=gk.t[64 * g:64 * g + 64, 0:cnt]),
                         reads=[gk.b], writes=[KC[g].bs[j]])
                for g in range(2):
                    P.op("tensor", lambda e: e.matmul(gk.t[0:cnt, 64 + 64 * g:128 + 64 * g], lhsT=hidT.t[:, 2 + g, 0:cnt],
                                                      rhs=W2[1].t[:], start=True, stop=True),
                         reads=[W2[1].b, hidT.b], writes=[gk.b])
                P.op("vector", lambda e: e.tensor_copy(out=vcst.t[0:cnt, :, :], in_=v3(gk.t[0:cnt, 64:192], 2)),
                     reads=[gk.b], writes=[vcst.b])
                for g in range(2):
                    n = n_start
                    while n < n_start + cnt:
                        m, p0 = divmod(n, 128)
                        ln = min(128 - p0, n_start + cnt - n)
                        s0 = n - n_start
                        P.dma("gpsimd", lambda e: e.dma_start(
                            out=VC[g].t[p0:p0 + ln, m, 0:64], in_=vcst.t[s0:s0 + ln, g, :]),
                            reads=[vcst.b], writes=[VC[g].bs[j]], stream=f"vc{g}")
                        n += ln
                yield

                marks[3] = len(P.cap) if P.cap is not None else 0
                _stage(4)
                otok = slice(384, 512)
                gq = gbank()
                gqs = gbank()
                for (g_, c0) in ((gq, 0), (gqs, 512)):
                    for m in range(4):
                        for kc in range(8):
                            P.op("tensor", lambda e: e.matmul(
                                g_.t[:, m * 128:(m + 1) * 128], lhsT=wfo.t[:, kc, c0 + m * 128:c0 + (m + 1) * 128],
                                rhs=hT.t[:, kc, otok], start=(kc == 0), stop=(kc == 7)),
                                reads=[wfo.b, hT.b], writes=[g_.b])
                _stage(4.1)
                P.op("vector", lambda e: e.tensor_copy(out=qraw[par].t[:], in_=v3(gq.t[:], 4)), reads=[gq.b], writes=[qraw[par].b])
                crope = bcast(RC.t[:, 384:512], 0, 4)
                srope = bcast(RS.t[:, 384:512], 0, 4)
                P.op("vector", lambda e: e.tensor_tensor(out=v3(rt1.t[:], 4), in0=v3(gq.t[:], 4), in1=crope, op=ALU.mult),
                     reads=[gq.b, RC.b], writes=[rt1.b])
                P.op("vector", lambda e: e.tensor_tensor(out=v3(rt2.t[:], 4), in0=v3(gqs.t[:], 4), in1=srope, op=ALU.mult),
                     reads=[gqs.b, RS.b], writes=[rt2.b])
                P.op("gpsimd", lambda e: e.tensor_tensor(out=qrot[par].t[:], in0=v3(rt1.t[:], 4), in1=v3(rt2.t[:], 4), op=ALU.add),
                     reads=[rt1.b, rt2.b], writes=[qrot[par].b])
                _stage(4.2)
                yield
                gt = gbank()
                for kc in range(8):
                    P.op("tensor", lambda e: e.matmul(gt.t[:, 0:24], lhsT=hT.t[:, kc, otok], rhs=wto.t[:, kc, 0:24],
                                                      start=(kc == 0), stop=(kc == 7)),
                         reads=[hT.b, wto.b], writes=[gt.b])
                P.op("scalar", lambda e: e.activation(out=gts[par].t[:], in_=gt.t[:, 0:24], func=AF.Sigmoid),
                     reads=[gt.b], writes=[gts[par].b])

                _stage(5)
                gc = gbank()
                for p2 in range(2):
                    P.op("tensor", lambda e: e.matmul(gc.t[:, p2 * 128:(p2 + 1) * 128],
                                                      lhsT=gl.t[:, p2 * 128:(p2 + 1) * 128], rhs=trile.t[:],
                                                      start=True, stop=True), reads=[gl.b, trile.b], writes=[gc.b])
                P.op("scalar", lambda e: e.activation(out=gEc.t[:], in_=v3(gc.t[:, 0:256], 2), func=AF.Exp),
                     reads=[gc.b], writes=[gEc.b])
                P.op("scalar", lambda e: e.activation(out=gEn.t[:], in_=v3(gc.t[:, 0:256], 2), func=AF.Exp, scale=-1.0),
                     reads=[gc.b], writes=[gEn.b])
                gg = gbank()
                for m in range(4):
                    for kc in range(8):
                        P.op("tensor", lambda e: e.matmul(
                            gg.t[:, m * 128:(m + 1) * 128], lhsT=wfo.t[:, kc, 1024 + m * 128:1024 + (m + 1) * 128],
                            rhs=hT.t[:, kc, otok], start=(kc == 0), stop=(kc == 7)),
                            reads=[wfo.b, hT.b], writes=[gg.b])
                for e2 in range(2):
                    rws = slice(64 * e2, 64 * e2 + 64)
                    P.op("vector", lambda e: e.scalar_tensor_tensor(
                        out=qdTz.t[rws, :, e2, :], in0=v3(gg.t[rws, 0:256], 2), scalar=0.125, in1=gEc.t[rws, :, :],
                        op0=ALU.mult, op1=ALU.mult), reads=[gg.b, gEc.b], writes=[qdTz.b])
                P.op("vector", lambda e: e.tensor_tensor(out=kdT.t[:], in0=v3(gg.t[:, 256:512], 2), in1=gEn.t[:], op=ALU.mult),
                     reads=[gg.b, gEn.b], writes=[kdT.b])
                yield
                ga2 = gbank()
                for h in range(4):
                    p2, e2 = divmod(h, 2)
                    P.op("tensor", lambda e: e.matmul(
                        ga2.t[:, h * 128:(h + 1) * 128], lhsT=kdT.t[:, p2, :],
                        rhs=qdTz.t[:, p2, e2, :], start=True, stop=True),
                        reads=[kdT.b, qdTz.b], writes=[ga2.b])
                P.op("vector", lambda e: e.tensor_tensor(out=Am.t[:], in0=v3(ga2.t[:], 4), in1=bcast(trimask.t[:], 0, 4),
                                                         op=ALU.mult), reads=[ga2.b, trimask.b], writes=[Am.b])
                go = gbank()
                for h in range(4):
                    p2, e2 = divmod(h, 2)
                    P.op("tensor", lambda e: e.matmul(
                        go.t[:, h * 128:(h + 1) * 128], lhsT=qdTz.t[:, p2, e2, :],
                        rhs=Sbf.t[:, p2, :], start=True, stop=False, skip_group_check=True),
                        reads=[qdTz.b, Sbf.b], writes=[go.b])
                    P.op("tensor", lambda e: e.matmul(
                        go.t[:, h * 128:(h + 1) * 128], lhsT=Am.t[:, h, :], rhs=vtm.t[:, 128 * h:128 * h + 128],
                        start=False, stop=True, skip_group_check=True), reads=[Am.b, vtm.b], writes=[go.b])
                gr2 = gbank()
                for kc in range(8):
                    P.op("tensor", lambda e: e.matmul(gr2.t[:], lhsT=hT.t[:, kc, otok], rhs=wto.t[:, kc, 24:536],
                                                      start=(kc == 0), stop=(kc == 7)),
                         reads=[hT.b, wto.b], writes=[gr2.b])
                P.op("scalar", lambda e: e.activation(out=gsr.t[:], in_=gr2.t[:], func=AF.Silu), reads=[gr2.b], writes=[gsr.b])
                P.op("vector", lambda e: e.tensor_tensor(out=v3(gsr.t[:], 4), in0=v3(gsr.t[:], 4), in1=bcast(gglat.t[:], 0, 4),
                                                         op=ALU.mult), reads=[gsr.b, gglat.b], writes=[gsr.b])
                yield
                for h in range(4):
                    P.op("scalar", lambda e: e.activation(out=junk.t[:, 0:128], in_=go.t[:, h * 128:(h + 1) * 128],
                                                          func=AF.Square, accum_out=gss.t[:, h:h + 1]),
                         reads=[go.b], writes=[junk.b, gss.b])
                P.op("vector", lambda e: e.tensor_scalar(out=gss.t[:, 0:4], in0=gss.t[:, 0:4], scalar1=1.0 / 128, scalar2=EPS,
                                                         op0=ALU.mult, op1=ALU.add), reads=[gss.b], writes=[gss.b])
                P.op("gpsimd", lambda e: e.tensor_tensor(out=gss.t[:, 4:8], in0=gss.t[:, 0:4], in1=nhalf.t[:, 0:4], op=ALU.pow),
                     reads=[gss.b, nhalf.b], writes=[gss.b])
                for h in range(4):
                    P.op("vector", lambda e: e.scalar_tensor_tensor(
                        out=ob.t[:, h * 128:(h + 1) * 128], in0=go.t[:, h * 128:(h + 1) * 128], scalar=gss.t[:, 4 + h:5 + h],
                        in1=gsr.t[:, h * 128:(h + 1) * 128], op0=ALU.mult, op1=ALU.mult),
                        reads=[go.b, gss.b, gsr.b], writes=[ob.b])
                P.dma("sync", lambda e: e.dma_start(out=oab[j * 128:(j + 1) * 128, 512:1024], in_=ob.t[:]),
                      reads=[ob.b], stream="ob")
                yield

            def partB(j):
                par = j % 2
                Tl = 4 * j + 3
                ntl = (8 * Tl + 6) // 128 + 1
                qr_, qo_, gt_ = qraw[par], qrot[par], gts[par]
                vt_, bf_, ca_ = validt[par], bigf[par], cmpA[par]
                for g in range(2):
                    rows = slice(64 * g, 64 * g + 64)
                    arow = slice(64, 128) if g == 0 else slice(0, 64)
                    kv = []
                    for m in range(ntl):
                        mk, mkb = None, []
                        if m == ntl - 1:
                            mk, mkb = ca_.t[:, 0, :], [ca_.b]
                        elif m == 0:
                            mk, mkb = cmpZ0.t[:], [cmpZ0.b]
                        kv.append((KC[g].t[:, m * 128:(m + 1) * 128], KC[g].bs[0:j + 1], qr_.t[:], [qr_.b], mk, mkb,
                                   VC[g].t[:, m, :], VC[g].bs[0:j + 1]))
                    yield from attention_branch(kv, ACC, 193, 2, gbankB)
                    for h in range(4):
                        O = ACC[h // 2]
                        c0 = (h % 2) * 193
                        P.op("vector", lambda e: e.tensor_scalar(
                            out=zt.t[:, h:h + 1], in0=O.t[:, c0 + 64:c0 + 65], scalar1=1e-30, scalar2=None, op0=ALU.add),
                            reads=[O.b], writes=[zt.b])
                    P.op("vector", lambda e: e.reciprocal(out=zt.t[:, 0:4], in_=zt.t[:, 0:4]), reads=[zt.b], writes=[zt.b])
                    for h in range(4):
                        O = ACC[h // 2]
                        c0 = (h % 2) * 193
                        if h == 0:
                            P.op("vector", lambda e: e.tensor_scalar(
                                out=imp.t[:], in0=O.t[:, c0 + 65:c0 + 193], scalar1=zt.t[:, 0:1], scalar2=None, op0=ALU.mult),
                                reads=[O.b, zt.b], writes=[imp.b])
                        else:
                            P.op("vector", lambda e: e.scalar_tensor_tensor(
                                out=imp.t[:], in0=O.t[:, c0 + 65:c0 + 193], scalar=zt.t[:, h:h + 1], in1=imp.t[:],
                                op0=ALU.mult, op1=ALU.add), reads=[O.b, zt.b, imp.b], writes=[imp.b])
                    P.op("vector", lambda e: e.tensor_tensor(out=coef.t[:, 0:4], in0=gt_.t[:, g * 12:g * 12 + 12:3],
                                                             in1=zt.t[:, 0:4], op=ALU.mult),
                         reads=[gt_.b, zt.b], writes=[coef.b])
                    for h in range(4):
                        O = ACC[h // 2]
                        c0 = (h % 2) * 193
                        P.op("vector", lambda e: e.tensor_scalar(
                            out=accc.t[:, h, :], in0=O.t[:, c0:c0 + 64], scalar1=coef.t[:, h:h + 1], scalar2=None, op0=ALU.mult),
                            reads=[O.b, coef.b], writes=[accc.b])
                    yield
                    P.op("vector", lambda e: e.tensor_tensor(out=sc.t[:], in0=imp.t[:], in1=bf_.t[:, 0, :], op=ALU.max),
                         reads=[imp.b, bf_.b], writes=[sc.b])
                    P.op("vector", lambda e: e.tensor_tensor(out=sc.t[:], in0=sc.t[:], in1=vt_.t[:, 0, :], op=ALU.mult),
                         reads=[sc.b, vt_.b], writes=[sc.b])
                    P.op("vector", lambda e: e.max(out=t16.t[:, 0:8], in_=sc.t[:]), reads=[sc.b], writes=[t16.b])
                    P.op("vector", lambda e: e.match_replace(out=wk.t[:], in_to_replace=t16.t[:, 0:8], in_values=sc.t[:],
                                                             imm_value=-1e9), reads=[sc.b, t16.b], writes=[wk.b])
                    P.op("vector", lambda e: e.max(out=t16.t[:, 8:16], in_=wk.t[:]), reads=[wk.b], writes=[t16.b])
                    P.op("vector", lambda e: e.scalar_tensor_tensor(out=wk.t[:], in0=sc.t[:], scalar=t16.t[:, 15:16],
                                                                    in1=vt_.t[:, 0, :], op0=ALU.is_ge, op1=ALU.mult),
                         reads=[sc.b, t16.b, vt_.b], writes=[wk.b])
                    for hh in range(2):
                        P.op("vector", lambda e: e.tensor_scalar(
                            out=B2.t[:, hh * 128:(hh + 1) * 128], in0=wk.t[:], scalar1=-NEG, scalar2=NEG,
                            op0=ALU.mult, op1=ALU.add), reads=[wk.b], writes=[B2.b])
                    yield
                    kv = []
                    for l in range(max(0, Tl - 4), Tl + 1):
                        mk, mkb = None, []
                        if j == 0 and l < 3:
                            mk, mkb = winj0.t[:, l, :], [winj0.b]
                        elif l == Tl:
                            mk, mkb = caus.t[:], [caus.b]
                        elif l == Tl - 4:
                            mk, mkb = winlo.t[:], [winlo.b]
                        rp = (l % 12) * 128
                        kv.append((KWT[g].t[:, rp:rp + 128], [KWT[g].bs[(l // 4) % 3]], qo_.t[:], [qo_.b], mk, mkb,
                                   VW.t[:, l % 12, g, :], [VW.bs[(l // 4) % 3]]))
                    yield from attention_branch(kv, [ACC[0]], 65, 4, gbankB)
                    Tb = gbankB()
                    P.op("tensor", lambda e: e.transpose(out=Tb.t[:, 0:128], in_=B2.t[:, 0:128], identity=identf.t[:]),
                         reads=[B2.b, identf.b], writes=[Tb.b])
                    P.op("tensor", lambda e: e.transpose(out=Tb.t[:, 128:256], in_=B2.t[:, 64:192], identity=identf.t[:]),
                         reads=[B2.b, identf.b], writes=[Tb.b])
                    nver = 1 if Tl < 32 else 2
                    for ver in range(nver):
                        tsel = (1 - ver) if g == 0 else ver
                        q_ = QA[g][ver]
                        P.op("scalar", lambda e: e.copy(out=q_.t[rows, :, :], in_=qo_.t[rows, :, :]),
                             reads=[qo_.b], writes=[q_.b])
                        P.op("vector", lambda e: e.tensor_copy(
                            out=q_.t[arow, :, :], in_=bcast(Tb.t[arow, tsel * 128:(tsel + 1) * 128], 0, 4)),
                            reads=[Tb.b], writes=[q_.b])
                    yield
                    kv = []
                    for i in range(Tl + 1):
                        mk, mkb = (caus.t[:], [caus.b]) if i == Tl else (None, [])
                        q_ = QA[g][0 if i < 32 else 1]
                        kv.append((KT[g].t[:, i * 128:(i + 1) * 128], [KT[g].bs[i // 4]], q_.t[:], [q_.b], mk, mkb,
                                   VS.t[:, i, g, :], [VS.bs[i // 4]]))
                    yield from attention_branch(kv, [ACC[1]], 65, 4, gbankB)
                    for k_ in range(2):
                        P.op("vector", lambda e: e.tensor_copy(out=osb.t[:, k_, :], in_=ACC[1 - k_].t[:, 0:260]),
                             reads=[ACC[1 - k_].b], writes=[osb.b])
                    for k_ in range(2):
                        brn = 1 + k_
                        P.op("vector", lambda e: e.tensor_scalar(
                            out=v3(zt.t[:, brn * 4:brn * 4 + 4], 4), in0=v3(osb.t[:, k_, :], 4)[:, :, 64:65],
                            scalar1=1e-30, scalar2=None, op0=ALU.add), reads=[osb.b], writes=[zt.b])
                    P.op("vector", lambda e: e.reciprocal(out=zt.t[:, 4:12], in_=zt.t[:, 4:12]), reads=[zt.b], writes=[zt.b])
                    for brn in (1, 2):
                        P.op("vector", lambda e: e.tensor_tensor(
                            out=coef.t[:, brn * 4:brn * 4 + 4], in0=gt_.t[:, g * 12 + brn:g * 12 + 12:3],
                            in1=zt.t[:, brn * 4:brn * 4 + 4], op=ALU.mult), reads=[gt_.b, zt.b], writes=[coef.b])
                    for h in range(4):
                        hc = (g * 4 + h) * 64
                        P.op("vector", lambda e: e.scalar_tensor_tensor(
                            out=acc.t[:], in0=osb.t[:, 0, h * 65:h * 65 + 64], scalar=coef.t[:, 4 + h:5 + h], in1=accc.t[:, h, :],
                            op0=ALU.mult, op1=ALU.add), reads=[osb.b, coef.b, accc.b], writes=[acc.b])
                        P.op("vector", lambda e: e.scalar_tensor_tensor(
                            out=oa.t[:, hc:hc + 64], in0=osb.t[:, 1, h * 65:h * 65 + 64], scalar=coef.t[:, 8 + h:9 + h],
                            in1=acc.t[:], op0=ALU.mult, op1=ALU.add), reads=[osb.b, coef.b, acc.b], writes=[oa.b])
                    yield
                P.dma("sync", lambda e: e.dma_start(out=oab[j * 128:(j + 1) * 128, 0:512], in_=oa.t[:]),
                      reads=[oa.b], stream="oa")
                yield

            def run_all(gen):
                try:
                    for _ in gen:
                        pass
                except _Stop:
                    pass

            def interleave(ga_, na, gb_, nb):
                ia = ib = 0
                da = db = False
                while not (da and db):
                    if db or (not da and ia * nb <= ib * na):
                        try:
                            next(ga_)
                            ia += 1
                        except StopIteration:
                            da = True
                    else:
                        try:
                            next(gb_)
                            ib += 1
                        except StopIteration:
                            db = True

            def capture(gen):
                P.cap = []
                run_all(gen)
                lst = P.cap
                P.cap = None
                return lst

            def merge(la, lb):
                na, nb = len(la), len(lb)
                ia = ib = 0
                while ia < na or ib < nb:
                    if ib >= nb or (ia < na and ia * nb * K_MR <= ib * na * 10):
                        P.replay(la[ia])
                        ia += 1
                    else:
                        P.replay(lb[ib])
                        ib += 1

            merge(capture(partA(0)), [])
            for j in range(K_NB):
                lb = capture(partB(j)) if K_B else []
                la = capture(partA(j + 1)) if j + 1 < K_NB else []
                if K_PIPE and K_SPLIT and la:
                    k_ = marks[K_SPLIT]
                    merge(la[:k_], lb)
                    merge(la[k_:], [])
                elif K_PIPE:
                    merge(la, lb)
                else:
                    merge(lb, [])
                    merge(la, [])

            P.drain("sync", [oa.b, ob.b])
            with nc.Block() as block:
                P.emit(block)

        gstate["n"] = 3
        with ExitStack() as ts:
            def tsb(name, shape, dt):
                return sb(name, shape, dt, 1, ts)

            X1 = sb("X1", [128, 8, DM], F32, 8, ts)
            bufA = tsb("bufA", [128, 8, 1024], BF16)
            oT = tsb("oT", [128, 8, 1024], BF16)
            BIG = tsb("BIG", [128, NFF * 1024], BF16)
            WR = sb("WR", [128, NFF * 512], BF16, NFF, ts)
            STG = [tsb(f"stg{i}", [128, 1024], F32) for i in range(6)]
            WBF = [tsb(f"wbf{i}", [128, 1024], BF16) for i in range(8)]
            oabts = [tsb(f"oabt{i}", [128, 1024], BF16) for i in range(2)]
            tf1 = tsb("tf1", [128, 512], F32)
            tf2 = tsb("tf2", [128, 512], F32)
            sg = tsb("sg", [128, 512], BF16)
            yts = [tsb(f"yt{i}", [128, DM], F32) for i in range(2)]
            gfint = tsb("gfint", [128, DM], F32)
            load("sync", gfint.t[:], gfin, "c20", gfint)
            gm = v3(BIG.t[:, 0:16 * 1024], 16)
            act = v3(BIG.t[:], NFF)
            WO = v3(WR.t[:, 0:8 * 1024], 8)
            WD = v3(WR.t[:], NFF)
            sst = {"s": 0, "w": 0, "c": 0}

            def stream_w(src, W, dst=None, dst_b=None):
                s_ = sst["s"] = (sst["s"] + 1) % 6
                stg = STG[s_]
                P.dma("sync", lambda e: e.dma_start(out=stg.t[:, 0:W], in_=src), writes=[stg.b], stream=f"st{s_}")
                wtb = None
                if dst is None:
                    w_ = sst["w"] = (sst["w"] + 1) % 8
                    wtb = WBF[w_]
                    dst, dst_b = wtb.t[:, 0:W], wtb.b
                sst["c"] += 1
                dst_bl = dst_b if isinstance(dst_b, list) else [dst_b]
                if sst["c"] % 3 == 0:
                    P.op("vector", lambda e: e.tensor_copy(out=dst, in_=stg.t[:, 0:W]), reads=[stg.b], writes=dst_bl)
                else:
                    P.op("scalar", lambda e: e.copy(out=dst, in_=stg.t[:, 0:W]), reads=[stg.b], writes=dst_bl)
                return wtb

            def final_tile(hf_, o8):
                o = hf_ * 8 + o8
                yt = yts[o8 % 2]
                rstd_from(X1.t[:, o8, :], X1.bs[o8], DM, 4, yt)
                P.op("vector", lambda e: e.scalar_tensor_tensor(out=yt.t[:], in0=X1.t[:, o8, :], scalar=st4.t[:, 9:10],
                                                                in1=gfint.t[:], op0=ALU.mult, op1=ALU.mult),
                     reads=[X1.bs[o8], st4.bs[4], gfint.b], writes=[yt.b])
                P.dma("sync", lambda e: e.dma_start(out=y[o * 128:(o + 1) * 128, :], in_=yt.t[:]),
                      reads=[yt.b], stream=f"y{o8 % 2}")

            for hf in range(2 if K_TAIL else 0):
                def t_chain(o8):
                    l = 4 * (hf * 8 + o8) + 3
                    P.dma("sync", lambda e: e.dma_start(out=X1.t[:, o8, :], in_=xl[l * 128:(l + 1) * 128, :]),
                          writes=[X1.bs[o8]], stream=f"x{o8 % 2}")
                    norm_chain(X1.t[:, o8, :], X1.bs[o8], o8 % 4)

                def t_tr(o8):
                    o = hf * 8 + o8
                    tsl = slice(o8 * 128, (o8 + 1) * 128)
                    norm_tr(o8 % 4, gmix, bufA.t[:, :, tsl], bufA.b)
                    oabt = oabts[o8 % 2]
                    P.dma("sync", lambda e: e.dma_start(out=oabt.t[:], in_=oab[o * 128:(o + 1) * 128, :]),
                          writes=[oabt.b], stream=f"oabt{o8 % 2}")
                    for k in range(8):
                        P.op("tensor", lambda e: e.transpose(out=TP.t[:, k * 128:(k + 1) * 128],
                                                             in_=oabt.t[:, k * 128:(k + 1) * 128], identity=ident.t[:]),
                             reads=[oabt.b, ident.b], writes=[TP.b])
                    P.op("scalar", lambda e: e.copy(out=oT.t[:, :, tsl], in_=v3(TP.t[:], 8)),
                         reads=[TP.b], writes=[oT.b])

                if hf > 0:
                    final_tile(hf - 1, 0)
                    final_tile(hf - 1, 1)
                for o8 in range(8):
                    t_chain(o8)
                    if hf > 0 and o8 + 2 < 8:
                        final_tile(hf - 1, o8 + 2)
                    if o8 >= 1:
                        t_tr(o8 - 1)
                t_tr(7)
                for m in range(16):
                    w = stream_w(w_gm[m], 1024)
                    for nb in range(2):
                        nsl = slice(nb * 512, (nb + 1) * 512)
                        g_ = gbank()
                        for kc in range(8):
                            P.op("tensor", lambda e, g_=g_, w=w, kc=kc, nsl=nsl: e.matmul(
                                g_.t[:], lhsT=v3(w.t[:], 8)[:, kc, :], rhs=bufA.t[:, kc, nsl],
                                start=(kc == 0), stop=(kc == 7)), reads=[w.b, bufA.b], writes=[g_.b])
                        P.op("scalar", lambda e, g_=g_, m=m, nsl=nsl: e.activation(out=gm[:, m, nsl], in_=g_.t[:],
                                                                                  func=AF.Sigmoid),
                             reads=[g_.b], writes=[BIG.b])
                for m in range(8):
                    wa = stream_w(w_up[m], 512)
                    wb = stream_w(w_up[8 + m], 512)
                    for nb in range(2):
                        nsl = slice(nb * 512, (nb + 1) * 512)
                        gA = gbank()
                        gB = gbank()
                        for (g_, w, k0) in ((gA, wa, 0), (gB, wb, 4)):
                            for k in range(4):
                                P.op("tensor", lambda e, g_=g_, w=w, k=k, k0=k0, nsl=nsl: e.matmul(
                                    g_.t[:], lhsT=v3(w.t[:, 0:512], 4)[:, k, :], rhs=oT.t[:, k0 + k, nsl],
                                    start=(k == 0), stop=(k == 3)), reads=[w.b, oT.b], writes=[g_.b])
                        P.op("vector", lambda e, gA=gA, m=m, nsl=nsl: e.tensor_tensor(out=tf1.t[:], in0=gA.t[:],
                                                                                     in1=gm[:, m, nsl], op=ALU.mult),
                             reads=[gA.b, BIG.b], writes=[tf1.b])
                        P.op("vector", lambda e, gB=gB, m=m, nsl=nsl: e.tensor_tensor(out=tf2.t[:], in0=gB.t[:],
                                                                                     in1=gm[:, 8 + m, nsl], op=ALU.mult),
                             reads=[gB.b, BIG.b], writes=[tf2.b])
                        P.op("gpsimd", lambda e, m=m, nsl=nsl: e.tensor_tensor(out=bufA.t[:, m, nsl], in0=tf1.t[:],
                                                                              in1=tf2.t[:], op=ALU.add),
                             reads=[tf1.b, tf2.b], writes=[bufA.b])
                for kc in range(8):
                    stream_w(w_out[kc], 1024, WO[:, kc, :], [WR.bs[2 * kc], WR.bs[2 * kc + 1]])
                for o8 in range(8):
                    tsl = slice(o8 * 128, (o8 + 1) * 128)
                    for nh in range(2):
                        nsl = slice(nh * 512, (nh + 1) * 512)
                        g_ = gbank()
                        for kc in range(8):
                            P.op("tensor", lambda e, g_=g_, kc=kc, tsl=tsl, nsl=nsl: e.matmul(
                                g_.t[:], lhsT=bufA.t[:, kc, tsl], rhs=WO[:, kc, nsl], start=(kc == 0), stop=(kc == 7)),
                                reads=[bufA.b, WR.bs[2 * kc], WR.bs[2 * kc + 1]], writes=[g_.b])
                        P.op("vector", lambda e, g_=g_, o8=o8, nsl=nsl: e.tensor_tensor(
                            out=X1.t[:, o8, nsl], in0=X1.t[:, o8, nsl], in1=g_.t[:], op=ALU.add),
                            reads=[X1.bs[o8], g_.b], writes=[X1.bs[o8]])
                for o8 in range(9):
                    if o8 < 8:
                        norm_chain(X1.t[:, o8, :], X1.bs[o8], o8 % 4)
                    if o8 >= 1:
                        tsl = slice((o8 - 1) * 128, o8 * 128)
                        norm_tr((o8 - 1) % 4, gffn, bufA.t[:, :, tsl], bufA.b)
                for m in range(NFF):
                    wg = stream_w(w_g[m], 1024)
                    wu = stream_w(w_u[m], 1024)
                    for nb in range(2):
                        nsl = slice(nb * 512, (nb + 1) * 512)
                        gG = gbank()
                        gU = gbank()
                        for (g_, w) in ((gG, wg), (gU, wu)):
                            for kc in range(8):
                                P.op("tensor", lambda e, g_=g_, w=w, kc=kc, nsl=nsl: e.matmul(
                                    g_.t[:], lhsT=v3(w.t[:], 8)[:, kc, :], rhs=bufA.t[:, kc, nsl],
                                    start=(kc == 0), stop=(kc == 7)), reads=[w.b, bufA.b], writes=[g_.b])
                        P.op("scalar", lambda e, gG=gG: e.activation(out=sg.t[:], in_=gG.t[:], func=AF.Silu),
                             reads=[gG.b], writes=[sg.b])
                        P.op("vector", lambda e, gU=gU, m=m, nsl=nsl: e.tensor_tensor(out=act[:, m, nsl], in0=gU.t[:],
                                                                                     in1=sg.t[:], op=ALU.mult),
                             reads=[gU.b, sg.b], writes=[BIG.b])
                for nh in range(2):
                    nsl = slice(nh * 512, (nh + 1) * 512)
                    for m in range(NFF):
                        stream_w(w_d[m][:, nsl], 512, WD[:, m, :], WR.bs[m])
                    for o8 in range(8):
                        tsl = slice(o8 * 128, (o8 + 1) * 128)
                        g_ = gbank()
                        for m in range(NFF):
                            P.op("tensor", lambda e, g_=g_, m=m, tsl=tsl: e.matmul(
                                g_.t[:], lhsT=act[:, m, tsl], rhs=WD[:, m, :], start=(m == 0), stop=(m == NFF - 1)),
                                reads=[BIG.b, WR.bs[m]], writes=[g_.b])
                        P.op("vector", lambda e, g_=g_, o8=o8, nsl=nsl: e.tensor_tensor(
                            out=X1.t[:, o8, nsl], in0=X1.t[:, o8, nsl], in1=g_.t[:], op=ALU.add),
                            reads=[X1.bs[o8], g_.b], writes=[X1.bs[o8]])
            if K_TAIL:
                for o8 in range(8):
                    final_tile(1, o8)
            P.drain("sync", [yts[0].b, yts[1].b])
            with nc.Block() as block:
                P.emit(block)
    return nc


def _bf(a):
    return np.ascontiguousarray(a.astype(np.float32)).astype(NPBF)


def _shared_inputs(inp):
    f32 = np.float32
    w_in = np.asarray(inp["w_in"][0], dtype=f32)
    sw = np.concatenate([np.arange(8, 16), np.arange(0, 8), np.arange(16, 64)])
    sw2 = np.concatenate([sw, 64 + sw])

    def kv(i):
        return w_in[:, C_KV + i * 128:C_KV + (i + 1) * 128]

    w_fm = np.concatenate([w_in[:, C_LR:C_LR + 16], kv(0), kv(1), kv(2), kv(2)[:, sw2], kv(4), kv(4)[:, sw2]], axis=1)
    w_tm = np.concatenate([kv(3), kv(5), w_in[:, C_KG:C_KG + 256], w_in[:, C_VG:C_VG + 512]], axis=1)
    qcols = []
    for m in range(4):
        qcols.append(w_in[:, 64 * m:64 * m + 64])
        qcols.append(w_in[:, 64 * (4 + m):64 * (4 + m) + 64])
    q = np.concatenate(qcols, axis=1)
    qsw = np.concatenate([c[:, sw] for c in qcols], axis=1)
    w_fo = np.concatenate([q, qsw, w_in[:, C_QG:C_QG + 256], w_in[:, C_KG:C_KG + 256]], axis=1)
    w_to = np.concatenate([w_in[:, C_GN:C_GN + 24], w_in[:, C_RG:C_RG + 512]], axis=1)

    def colT(v):
        return np.ascontiguousarray(np.asarray(v, dtype=f32).reshape(8, 128).T)

    def w1l(w1):
        a = np.asarray(w1, dtype=f32).reshape(32, 64, 128).transpose(1, 0, 2).reshape(64, 32 * 128)
        return np.ascontiguousarray(np.concatenate([a, a], axis=0))

    def peT(pe):
        a = np.asarray(pe, dtype=f32).T
        return np.ascontiguousarray(np.concatenate([a, a], axis=0))

    p = np.arange(128)
    le = (p[:, None] <= p[None, :])
    n_all = (np.arange(4)[None, :, None] * 128 + p[:, None, None])
    blk = np.arange(128)[None, None, :]
    ov = ((16 * n_all < 64 * blk + 64) & (16 * n_all + 32 > 64 * blk)).astype(f32).reshape(128, 512)
    tl = np.arange(SEQ)
    epat = (np.arange(64)[:, None] == ((tl // 64) % 64)[None, :]).astype(f32)

    def tiles(w, nk, nm):
        return np.ascontiguousarray(np.asarray(w, dtype=f32).reshape(nk, 128, nm, 128).transpose(2, 1, 0, 3)
                                    .reshape(nm, 128, nk * 128))

    sh = {
        "w_fm": w_fm, "w_tm": w_tm, "w_fo": w_fo, "w_to": w_to,
        "gmixT": colT(inp["norm_mix"][0]), "gffnT": colT(inp["norm_ffn"][0]),
        "gfin": np.tile(np.asarray(inp["norm_final"], dtype=f32)[None, :], (128, 1)),
        "ggla": np.tile(np.asarray(inp["gla_norm"][0], dtype=f32)[None, :], (128, 1)),
        "w1k": w1l(inp["cmp_k_w1"][0]), "w1v": w1l(inp["cmp_v_w1"][0]),
        "w2k": np.asarray(inp["cmp_k_w2"][0], dtype=f32), "w2v": np.asarray(inp["cmp_v_w2"][0], dtype=f32),
        "pekT": peT(inp["cmp_pe_k"][0]), "pevT": peT(inp["cmp_pe_v"][0]),
        "w2aug": np.concatenate([np.asarray(inp["gla_gate_w2"][0], dtype=f32),
                                 np.asarray(inp["gla_gate_b"][0], dtype=f32)[None, :]], axis=0),
        "ident": _bf(np.eye(128)),
        "identf": np.eye(128, dtype=f32),
        "caus": _bf(np.where(le, 0.0, NEG)),
        "winlo": _bf(np.where(p[:, None] > p[None, :], 0.0, NEG)),
        "trile": np.where(le, -1.0 / 16, 0.0).astype(f32),
        "su": np.where(p[:, None] > p[None, :], -1.0 / 16, 0.0).astype(f32),
        "trimask": _bf(le.astype(f32)),
        "negcol": np.full((128, 1), -1.0 / 16, dtype=f32),
        "ov": _bf(ov), "epat": _bf(epat),
        "w_gm": tiles(w_in[:, C_GM:C_GM + 2048], 8, 16),
        "w_up": np.concatenate([tiles(inp["w_up_nsa"][0], 4, 8), tiles(inp["w_up_gla"][0], 4, 8)], axis=0),
        "w_out": np.ascontiguousarray(np.asarray(inp["w_out"][0], dtype=f32).reshape(8, 128, 1024)),
        "w_g": tiles(inp["w_ffn_gate"][0], 8, NFF), "w_u": tiles(inp["w_ffn_up"][0], 8, NFF),
        "w_d": np.ascontiguousarray(np.asarray(inp["w_ffn_down"][0], dtype=f32).reshape(NFF, 128, 1024)),
    }
    return {k: np.ascontiguousarray(v) for k, v in sh.items()}


def _core_inputs(x, b, c):
    f32 = np.float32
    shf = 3 - c
    off = 128 * shf
    xl = np.zeros((SEQ, DM), dtype=f32)
    xl[off:] = x[b, :SEQ - off]
    half = 8
    inv_freq = (np.float32(500000.0) ** (-np.arange(half, dtype=f32) / np.float32(half))).astype(f32)
    pos = (np.arange(SEQ) - off).astype(f32)
    ang = (pos[:, None] * inv_freq[None, :]).astype(f32)
    cs, sn = np.cos(ang).astype(f32), np.sin(ang).astype(f32)
    C64 = np.ones((64, SEQ), dtype=f32)
    S64 = np.zeros((64, SEQ), dtype=f32)
    C64[0:8] = cs.T
    C64[8:16] = cs.T
    S64[0:8] = -sn.T
    S64[8:16] = sn.T
    C128 = np.concatenate([C64, C64], axis=0)
    S128 = np.concatenate([S64, S64], axis=0)
    ropeC = np.ascontiguousarray(C128.reshape(128, NB, 512).transpose(1, 0, 2))
    ropeS = np.ascontiguousarray(S128.reshape(128, NB, 512).transpose(1, 0, 2))
    q = np.arange(128)[:, None, None]
    j = np.arange(NB)[None, :, None]
    blk = np.arange(128)[None, None, :]
    t_loc = 128 * (4 * j + 3) + q
    cur = t_loc // 64
    valid = ((blk >= 2 * shf) & (64 * blk <= t_loc)).astype(f32)
    forced = ((blk == 2 * shf) | (blk == cur) | (blk == cur - 1))
    bigf = np.where(forced, 1.0e4, 0.0).astype(f32)
    pn = np.arange(128)[:, None, None]
    jq = np.arange(NB)[None, :, None]
    qq = np.arange(128)[None, None, :]
    Tl = 4 * jq + 3
    m_last = (8 * Tl + 6) // 128
    n = 128 * m_last + pn
    okc = (n >= 8 * shf) & (16 * n + 31 <= 128 * Tl + qq)
    cmpA = np.where(okc, 0.0, NEG).astype(f32)
    cmpZ0 = np.where((np.arange(128)[:, None] >= 8 * shf) & (np.arange(128)[None, :] >= 0), 0.0, NEG).astype(f32)
    winj0 = np.zeros((128, 3, 128), dtype=f32)
    for l in range(3):
        if l < shf:
            winj0[:, l, :] = NEG
    return {
        "xl": xl, "ropeC": ropeC, "ropeS": ropeS,
        "validtab": _bf(valid.reshape(128, NB * 128)), "bigf": _bf(bigf.reshape(128, NB * 128)),
        "cmpA": _bf(cmpA.reshape(128, NB * 128)), "cmpZ0": _bf(cmpZ0), "winj0": _bf(winj0.reshape(128, 384)),
    }


_NC_CACHE = {}


def kernel(**inputs):
    x = np.asarray(inputs["x"], dtype=np.float32)
    shared = _shared_inputs(inputs)
    in_maps = []
    for core in range(8):
        b, c = divmod(core, 4)
        m = dict(shared)
        m.update(_core_inputs(x, b, c))
        in_maps.append(m)
    if "nc" not in _NC_CACHE:
        _NC_CACHE["nc"] = build_nc()
    nc = _NC_CACHE["nc"]
    res = run_bass_kernel_spmd(nc, in_maps[:K_CORES], core_ids=list(range(K_CORES)))
    out = np.zeros((2, SEQ, DM), dtype=np.float32)
    for core in range(K_CORES):
        b, c = divmod(core, 4)
        yc = np.asarray(res.results[core]["y"], dtype=np.float32)
        for j in range(NB):
            T = 4 * j + c
            out[b, T * 128:(T + 1) * 128] = yc[j * 128:(j + 1) * 128]
        if DEBUG:
            _dbg[core] = np.asarray(res.results[core]["oab"])
    return out
```
